# Optimizing a Trainium2 kernel written in Bass

```python
import jax, jax.numpy as jnp
from jax import lax
import numpy as np

D_MODEL = 2048
BATCH = 4
SEQ = 2048
DEPTH = 2
DEC_BATCH = 128
DEC_SEQ = 8
PAST_LEN = 16384
PAGE_SIZE = 128

D_A = D_MODEL
D_B = D_MODEL
CONV_A_WIDTH = 3
CONV_B_WIDTH = 31
D_FF = ((8 * D_MODEL // 3 + 255) // 256) * 256
RMS_EPS = 1e-6
LN_EPS = 1e-5
IN_SPLITS = [D_A, 2 * D_A, 3 * D_A, 3 * D_A + D_B, 3 * D_A + 2 * D_B, 3 * D_A + 2 * D_B + D_MODEL]
N_IN = 3 * D_A + 2 * D_B + 2 * D_MODEL

kernel_name = "hybrid_shortconv_conformerconv_decode_step"


def _rmsnorm(x, g):
    xf = x.astype(jnp.float32)
    y = xf * lax.rsqrt(jnp.mean(xf * xf, axis=-1, keepdims=True) + RMS_EPS)
    return (y * g.astype(jnp.float32)).astype(x.dtype)


def _layernorm(x, g, b):
    xf = x.astype(jnp.float32)
    mu = jnp.mean(xf, axis=-1, keepdims=True)
    xc = xf - mu
    var = jnp.mean(xc * xc, axis=-1, keepdims=True)
    y = xc * lax.rsqrt(var + LN_EPS) * g.astype(jnp.float32) + b.astype(jnp.float32)
    return y.astype(x.dtype)


def _causal_depthwise(x, buf, w):
    width, ch = w.shape
    xp = jnp.concatenate([buf.astype(x.dtype), x], axis=1)
    y = lax.conv_general_dilated(
        xp, w.astype(x.dtype)[:, None, :], window_strides=(1,), padding='VALID',
        dimension_numbers=('NWC', 'WIO', 'NWC'), feature_group_count=ch)
    return y, xp[:, xp.shape[1] - (width - 1):]


def _layer(x, buf_a, buf_b, norm_mix_g, w_in, conv_a_w, w_out_a, conv_b_w, conv_b_bias,
           ln_b_g, ln_b_b, w_out_b, w_o, norm_ffn_g, w_gate, w_up, w_down):
    h = _rmsnorm(x, norm_mix_g)
    proj = jnp.einsum('btd,dn->btn', h, w_in)
    b_a, c_a, v_a, glu_a, glu_b, gate_a, gate_b = jnp.split(proj, IN_SPLITS, axis=-1)
    conv_a, new_a = _causal_depthwise(c_a * v_a, buf_a, conv_a_w)
    y_a = jnp.einsum('btc,cd->btd', b_a * conv_a, w_out_a)
    u = glu_a * jax.nn.sigmoid(glu_b)
    conv_b, new_b = _causal_depthwise(u, buf_b, conv_b_w)
    z = jax.nn.silu(_layernorm(conv_b + conv_b_bias, ln_b_g, ln_b_b))
    y_b = jnp.einsum('btc,cd->btd', z, w_out_b)
    merged = jax.nn.sigmoid(gate_a) * y_a + jax.nn.sigmoid(gate_b) * y_b
    x = x + jnp.einsum('btd,de->bte', merged, w_o)
    h2 = _rmsnorm(x, norm_ffn_g)
    ff = jax.nn.silu(jnp.einsum('btd,df->btf', h2, w_gate)) * jnp.einsum('btd,df->btf', h2, w_up)
    x = x + jnp.einsum('btf,fd->btd', ff, w_down)
    return x, new_a, new_b


def setup_inputs(seed: int = 0) -> dict:
    key = jax.random.key(seed)
    ks = jax.random.split(key, 20)
    f32 = jnp.float32
    nrm = lambda k, shape, scale: jax.random.normal(k, shape, f32) * scale
    return {
        "x_prompt": nrm(ks[0], (BATCH, SEQ, D_MODEL), 1.0),
        "x_sample": nrm(ks[1], (DEC_BATCH, DEC_SEQ, D_MODEL), 1.0),
        "state_conv_a": nrm(ks[2], (DEPTH, DEC_BATCH, CONV_A_WIDTH - 1, D_A), 0.5),
        "state_conv_b": nrm(ks[3], (DEPTH, DEC_BATCH, CONV_B_WIDTH - 1, D_B), 0.5),
        "norm_mix_g": 1.0 + nrm(ks[4], (DEPTH, D_MODEL), 0.02),
        "w_in": nrm(ks[5], (DEPTH, D_MODEL, N_IN), D_MODEL ** -0.5),
        "conv_a_w": nrm(ks[6], (DEPTH, CONV_A_WIDTH, D_A), CONV_A_WIDTH ** -0.5),
        "w_out_a": nrm(ks[7], (DEPTH, D_A, D_MODEL), D_A ** -0.5),
        "conv_b_w": nrm(ks[8], (DEPTH, CONV_B_WIDTH, D_B), CONV_B_WIDTH ** -0.5),
        "conv_b_bias": nrm(ks[9], (DEPTH, D_B), 0.02),
        "ln_b_g": 1.0 + nrm(ks[10], (DEPTH, D_B), 0.02),
        "ln_b_b": nrm(ks[11], (DEPTH, D_B), 0.02),
        "w_out_b": nrm(ks[12], (DEPTH, D_B, D_MODEL), D_B ** -0.5),
        "w_o": nrm(ks[13], (DEPTH, D_MODEL, D_MODEL), D_MODEL ** -0.5),
        "norm_ffn_g": 1.0 + nrm(ks[14], (DEPTH, D_MODEL), 0.02),
        "w_gate": nrm(ks[15], (DEPTH, D_MODEL, D_FF), D_MODEL ** -0.5),
        "w_up": nrm(ks[16], (DEPTH, D_MODEL, D_FF), D_MODEL ** -0.5),
        "w_down": nrm(ks[17], (DEPTH, D_FF, D_MODEL), D_FF ** -0.5),
        "final_norm_g": 1.0 + nrm(ks[18], (D_MODEL,), 0.02),
    }


def reference(x_prompt, x_sample, state_conv_a, state_conv_b, norm_mix_g, w_in, conv_a_w, w_out_a,
              conv_b_w, conv_b_bias, ln_b_g, ln_b_b, w_out_b, w_o, norm_ffn_g, w_gate, w_up, w_down,
              final_norm_g):
    xp, xs = x_prompt, x_sample
    pa, pb, sa, sb = [], [], [], []
    for l in range(DEPTH):
        params = (norm_mix_g[l], w_in[l], conv_a_w[l], w_out_a[l], conv_b_w[l], conv_b_bias[l],
                  ln_b_g[l], ln_b_b[l], w_out_b[l], w_o[l], norm_ffn_g[l], w_gate[l], w_up[l], w_down[l])
        zero_a = jnp.zeros((xp.shape[0], CONV_A_WIDTH - 1, D_A), xp.dtype)
        zero_b = jnp.zeros((xp.shape[0], CONV_B_WIDTH - 1, D_B), xp.dtype)
        xp, na, nb = _layer(xp, zero_a, zero_b, *params)
        pa.append(na)
        pb.append(nb)
        xs, ma, mb = _layer(xs, state_conv_a[l], state_conv_b[l], *params)
        sa.append(ma)
        sb.append(mb)
    y_prompt = _rmsnorm(xp, final_norm_g)
    y_sample = _rmsnorm(xs, final_norm_g)
    return (y_prompt, y_sample, jnp.stack(pa), jnp.stack(pb), jnp.stack(sa), jnp.stack(sb))
```

```python
import contextlib
import numpy as np
import concourse.bass as bass
import concourse.mybir as mybir
from concourse.bass_utils import run_bass_kernel_spmd

F32 = mybir.dt.float32
F32R = mybir.dt.float32r
BF16 = mybir.dt.bfloat16
AF = mybir.ActivationFunctionType
ALU = mybir.AluOpType

RMS_EPS = 1e-6
LN_EPS = 1e-5
WA = 3
WB = 31
DEC = 8
NL = 2
NCORES = 8


class Cfg:
    def __init__(self, D=2048, DFF=5632, SEQ=2048, BATCH=4, NSAMP=128, NPC=1056, T=592, NSLOT=5, NDVE=(26, 26), NPE=(0, 0)):
        self.D, self.DFF, self.SEQ, self.BATCH, self.NSAMP = D, DFF, SEQ, BATCH, NSAMP
        self.KC, self.FC = D // 128, DFF // 128
        self.NPC, self.T, self.H = NPC, T, T // 2
        self.NS = NSAMP // NCORES
        self.TSAMP = self.NS * DEC
        self.NCOLS = NPC + self.TSAMP
        assert self.NCOLS == 2 * T and T % 2 == 0 and self.H <= 512
        self.tiles = [(T, 0), (NPC - T, self.TSAMP)]
        assert NPC - T >= WB - 1
        self.HALO = 2 * NPC - SEQ
        assert self.HALO >= 2 * (WB - 1)
        self.NSLOT = NSLOT
        self.NDVE = NDVE
        self.NPE = NPE
        KC, FC = self.KC, self.FC
        groups = []
        self.FCH = FC // 2
        for j in range(KC):
            groups.append(("A1a", j, 256, KC))
            groups.append(("A1b", j, 256, KC))
            groups.append(("A1c", j, 128, KC))
        for j in range(KC):
            groups.append(("A3a", j, 256, KC))
            groups.append(("A3b", j, 256, KC))
        self.A4G = min(2, KC)
        for q in range(KC // self.A4G):
            groups.append(("A4", q, 128 * self.A4G, KC))
        for f in range(FC):
            groups.append(("F1", f, 256, KC))
        for j in range(KC):
            groups.append(("F2", 2 * j, 128, self.FCH))
            groups.append(("F2", 2 * j + 1, 128, self.FCH))
        self.groups = groups
        offs, o = [], 0
        for (_, _, nc_, kcm) in groups:
            offs.append(o)
            o += nc_ * kcm
        self.goffs, self.WTOT = offs, o
        self.SLOT_L = max(nc_ * kcm for (_, _, nc_, kcm) in groups)
        c = {}
        o = 0
        for l in range(NL):
            for nm, n in (("g1", KC), ("wa", KC * WA), ("wb", KC * WB), ("bb", KC), ("lg", KC), ("lb", KC), ("g2", KC)):
                c[(nm, l)] = o
                o += n
        c[("gf", 0)] = o
        o += KC
        c[("ident", 0)] = o
        o += 128
        self.coff, self.NCONST = c, o


FULL = Cfg()


class Res:
    __slots__ = ("name", "w", "r")

    def __init__(self, name):
        self.name, self.w, self.r = name, None, []


class Chan:
    __slots__ = ("name", "n", "sem")

    def __init__(self, name):
        self.name, self.n, self.sem = name, 0, None


class Op:
    __slots__ = ("eng", "fn", "deps", "chan", "cnt", "need", "relax", "idx")

    def __init__(self, eng, fn, chan):
        self.eng, self.fn, self.chan = eng, fn, chan
        self.deps, self.cnt, self.need = (), None, False
        self.relax, self.idx = False, 0


def _skip_dep(op, d):
    if d.chan is not None or d.eng != op.eng:
        return False
    if op.eng == "pe":
        return True
    return op.relax and d.relax and (op.idx - d.idx) >= 2


ENGS = ("pe", "act", "dve", "pool", "sp")


class Sched:
    def __init__(self):
        self.ops = {e: [] for e in ENGS}
        self.chans = []

    def chan(self, name):
        c = Chan(name)
        self.chans.append(c)
        return c

    def add(self, eng, fn, reads=(), writes=(), chan=None, ndma=1, relax=False):
        op = Op(eng, fn, chan)
        op.relax, op.idx = relax, len(self.ops[eng])
        deps = set()
        for r in reads:
            if r.w is not None:
                deps.add(r.w)
        for w in writes:
            if w.w is not None:
                deps.add(w.w)
            deps.update(w.r)
        for r in reads:
            r.r.append(op)
        for w in writes:
            w.w = op
            w.r = []
        deps.discard(op)
        op.deps = deps
        if chan is not None:
            chan.n += ndma
            op.cnt = 16 * chan.n
            op.need = True
        self.ops[eng].append(op)
        return op

    def finalize(self):
        for e in ENGS:
            for op in self.ops[e]:
                for d in op.deps:
                    if _skip_dep(op, d):
                        continue
                    d.need = True
        for e in ENGS:
            c = 0
            for op in self.ops[e]:
                if op.chan is None and op.need:
                    c += 1
                    op.cnt = c

    def emit(self, eng, e, esem, final_waits=()):
        waited = {}
        for op in self.ops[eng]:
            w = {}
            for d in op.deps:
                if _skip_dep(op, d):
                    continue
                if d.chan is not None:
                    key, sem = ("c", id(d.chan)), d.chan.sem
                else:
                    key, sem = ("e", d.eng), esem[d.eng]
                if d.cnt > w.get(key, (None, 0))[1]:
                    w[key] = (sem, d.cnt)
            for key, (sem, v) in w.items():
                if waited.get(key, 0) < v:
                    e.wait_ge(sem, v)
                    waited[key] = v
            inst = op.fn(e)
            if op.chan is not None:
                for ii in (inst if isinstance(inst, list) else [inst]):
                    ii.then_inc(op.chan.sem, 16)
            elif op.need:
                inst.then_inc(esem[eng], 1)
        for (sem, v) in final_waits:
            e.wait_ge(sem, v)


def build(cfg):
    KC, FC, T, H, NS = cfg.KC, cfg.FC, cfg.T, cfg.H, cfg.NS
    D = cfg.D
    nc = bass.Bass("TRN2", target_bir_lowering=False)
    S = Sched()

    def dram(name, shape, kind):
        return nc.dram_tensor(name, shape, F32, kind=kind).ap()

    xin = dram("xin", [D, cfg.NCOLS], "ExternalInput")
    sa_in = dram("sa", [NL, KC, 128, NS * 10], "ExternalInput")
    sb_in = dram("sb", [NL, KC, 128, NS * 38], "ExternalInput")
    wts = dram("wts", [NL, 128, cfg.WTOT], "ExternalInput")
    cst_in = dram("cst", [128, cfg.NCONST], "ExternalInput")
    yout = dram("yout", [D, cfg.NCOLS], "ExternalOutput")
    ncap = dram("ncap", [128, NL * KC * 2], "ExternalOutput")
    ncbp = dram("ncbp", [128, NL * KC * 30], "ExternalOutput")
    ncas = dram("ncas", [NL, KC, 128, NS * 10], "ExternalOutput")
    ncbs = dram("ncbs", [NL, KC, 128, NS * 38], "ExternalOutput")

    hs = nc.dram_tensor("hs", [NL, KC, 128, NS * DEC], F32, kind="ExternalOutput").ap()
    es = contextlib.ExitStack()
    with es:
        def sb(name, shape, dt=F32):
            return es.enter_context(nc.sbuf_tensor(name, shape, dt))

        if getattr(cfg, "PAD", 0):
            sb("pad", [128, cfg.PAD // 4])
        xT = sb("xT", [128, KC, T])
        hb = sb("hb", [128, KC, T], BF16)
        U = sb("U", [128, 2 * KC * T])
        slots = [sb(f"slot{i}", [128, cfg.SLOT_L], BF16) for i in range(cfg.NSLOT)]
        cst = sb("cst_sb", [128, cfg.NCONST])
        carry_a = sb("carry_a", [128, NL * KC * 2])
        carry_b = sb("carry_b", [128, NL * KC * 30])
        cvp = sb("cvp", [128, 2 + T])
        cvs = sb("cvs", [128, NS * 10])
        ubp = [sb(f"ubp{i}", [128, 30 + T]) for i in range(2)]
        ubs = [sb(f"ubs{i}", [128, NS * 38]) for i in range(2)]
        tmpA = sb("tmpA", [128, T])
        tmpB = sb("tmpB", [128, T])
        acc = sb("acc", [128, T])
        accB = sb("accB", [128, T])
        tmpBa = sb("tmpBa", [128, T])
        _tb = tmpBa[:, :].bitcast(BF16)
        sq = [_tb[:, 0:T], _tb[:, T:2 * T]]
        NDG = 6
        dg = [sb(f"dg{i}", [128, 128]) for i in range(NDG)]
        r_dg = [Res(f"dg{i}") for i in range(NDG)]
        rt, rstd, meanb = tmpA, tmpB, acc
        ones_bf = sb("ones_bf", [128, 128], BF16)
        ones_f = sb("ones_f", [128, 128])
        PB = [es.enter_context(nc.psum_tensor(f"pb{i}", [128, 2, 512], F32)) for i in range(4)]

        r_x = [Res(f"x{j}") for j in range(KC)]
        r_h = [Res(f"h{j}") for j in range(KC)]
        r_u = [Res(f"u{b}") for b in range(4 * KC)]
        r_slot = [Res(f"slot{i}") for i in range(cfg.NSLOT)]
        r_cst = Res("cst")
        r_ca, r_cb_ = Res("carry_a"), Res("carry_b")
        r_cvp, r_cvs = Res("cvp"), Res("cvs")
        r_ubp = [Res("ubp0"), Res("ubp1")]
        r_ubs = [Res("ubs0"), Res("ubs1")]
        rA = [Res("tmpA_p"), Res("tmpA_s")]
        r_tB = Res("tmpB")
        rAcc = [Res("acc_p"), Res("acc_s")]
        rAccB = Res("accB")
        r_sq = [Res("sq0"), Res("sq1")]
        r_ones = Res("ones")
        r_pb = [Res(f"pb{i}") for i in range(4)]

        ch_slot = [S.chan(f"slot{i}") for i in range(cfg.NSLOT)]
        ch_c = S.chan("cst")
        ch_y = [S.chan(f"y{k}") for k in range(KC)]
        ch_x = [S.chan(f"x{k}") for k in range(KC)]
        ch_cvs_l, ch_cvs_s = S.chan("cvs_l"), S.chan("cvs_s")
        ch_ubs_l = [S.chan("ubs_l0"), S.chan("ubs_l1")]
        ch_ubs_s = [S.chan("ubs_s0"), S.chan("ubs_s1")]
        ch_fin = S.chan("fin")
        ch_hs_s = [S.chan(f"hs_s{i}") for i in range(NDG)]
        ch_hl = S.chan("hs_l")
        r_hs = [[Res(f"hs{l}_{j}") for j in range(KC)] for l in range(NL)]
        KH = WB - DEC

        Ubf = U[:, :].bitcast(BF16)

        def blk(b):
            return Ubf[:, b * T:(b + 1) * T]

        def v2(ap):
            return ap.rearrange("p (a n) -> p a n", a=2)

        def ps(pb):
            return pb[:, :, 0:H]

        def cbf(j):
            return U[:, KC * T + j * T: KC * T + (j + 1) * T]

        def r_cbf(j):
            return [r_u[2 * KC + 2 * j], r_u[2 * KC + 2 * j + 1]]

        def yv(j):
            return U[:, j * T:(j + 1) * T]

        eps_t = sb("eps_t", [128, 2])
        r_eps = Res("eps")
        dummy = sb("act_dummy_t", [128, 2])

        def act_preload(func):
            S.add("act", lambda e: e.activation(out=dummy[:, 0:1], in_=eps_t[:, 0:1], func=func), reads=[r_eps])

        def cc(nm, l, i, n=1):
            if nm == "eps":
                return eps_t[:, i:i + 1]
            o = cfg.coff[(nm, l)] + i
            return cst[:, o:o + n]

        S.add("sp", lambda e: e.dma_start(out=cst[:, :], in_=cst_in[:, :]), writes=[r_cst], chan=ch_c)
        S.add("dve", lambda e: e.memset(ones_bf[:, :], 1.0 / D), writes=[r_ones])
        S.add("dve", lambda e: e.memset(ones_f[:, :], 1.0 / D), writes=[r_ones])
        S.add("dve", lambda e: e.memset(carry_a[:, :], 0.0), writes=[r_ca])
        S.add("dve", lambda e: e.memset(carry_b[:, :], 0.0), writes=[r_cb_])
        S.add("dve", lambda e: e.memset(eps_t[:, 0:1], RMS_EPS), writes=[r_eps])
        S.add("dve", lambda e: e.memset(eps_t[:, 1:2], LN_EPS), writes=[r_eps])
        act_preload(AF.Ln)

        gseq = []
        for _ti in range(len(cfg.tiles)):
            for l in range(NL):
                for gi in range(len(cfg.groups)):
                    gseq.append((l, gi))
        ws = {"issued": 0, "next": 0}

        def ws_issue(after=()):
            g = ws["issued"]
            if g >= len(gseq):
                return
            l, gi = gseq[g]
            _, _, ncols, kcm = cfg.groups[gi]
            L = ncols * kcm
            off = cfg.goffs[gi]
            s = g % cfg.NSLOT
            S.add("pool", lambda e, s=s, l=l, off=off, L=L: e.dma_start(out=slots[s][:, 0:L], in_=wts[l, :, off:off + L]),
                  reads=list(after), writes=[r_slot[s]], chan=ch_slot[s])
            ws["issued"] += 1

        def ws_acquire(kind, idx):
            g = ws["next"]
            l, gi = gseq[g]
            assert cfg.groups[gi][0] == kind and cfg.groups[gi][1] == idx, (cfg.groups[gi], kind, idx)
            while ws["issued"] <= g:
                ws_issue()
            return g % cfg.NSLOT, cfg.groups[gi][2], cfg.groups[gi][3]

        def ws_release():
            ws["next"] += 1
            while ws["issued"] < min(len(gseq), ws["next"] + cfg.NSLOT):
                ws_issue()

        for _ in range(min(2, cfg.NSLOT)):
            ws_issue()

        pbrot = {"i": 0, "n": 3}

        def next_pb():
            i = pbrot["i"] % pbrot["n"]
            pbrot["i"] += 1
            return i

        def mm(pbi, s, ncols, kcm, n, in_fn, in_res, kc0=0, first=True, last=True, fine=False):
            if fine:
                in_res = list(in_res)
                for kc in range(kcm):
                    def fk(e, kc=kc):
                        li = None
                        lhsT = slots[s][:, kc * ncols + n * 128: kc * ncols + (n + 1) * 128]
                        for hh in range(2):
                            li = e.matmul(PB[pbi][:, hh, 0:H], lhsT, in_fn(kc0 + kc, hh),
                                          start=(first and kc == 0), stop=(last and kc == kcm - 1))
                        return li
                    S.add("pe", fk, reads=[r_slot[s], in_res[kc0 + kc]], writes=[r_pb[pbi]])
                return

            def fn(e):
                li = None
                for kc in range(kcm):
                    lhsT = slots[s][:, kc * ncols + n * 128: kc * ncols + (n + 1) * 128]
                    for hh in range(2):
                        li = e.matmul(PB[pbi][:, hh, 0:H], lhsT, in_fn(kc0 + kc, hh),
                                      start=(first and kc == 0), stop=(last and kc == kcm - 1))
                return li
            S.add("pe", fn, reads=[r_slot[s]] + list(in_res), writes=[r_pb[pbi]])

        def h_in(kc, hh):
            return hb[:, kc, hh * H:(hh + 1) * H]

        def blk_in(base):
            return lambda kc, hh: blk(base + kc)[:, hh * H:(hh + 1) * H]

        def stats_step(j):
            sj = j % 2
            S.add("act", lambda e, j=j, sj=sj: e.activation(out=sq[sj], in_=xT[:, j, :], func=AF.Square),
                  reads=[r_x[j]], writes=[r_sq[sj]])

            def fn(e, j=j, sj=sj):
                li = None
                for hh in range(2):
                    li = e.matmul(PB[3][:, hh, 0:H], ones_bf[:, :], sq[sj][:, hh * H:(hh + 1) * H],
                                  start=(j == 0), stop=(j == KC - 1))
                return li
            S.add("pe", fn, reads=[r_sq[sj], r_ones], writes=[r_pb[3]])

        def rms_finish(nxt=None):
            S.add("act", lambda e: e.activation(out=v2(rt[:, :]), in_=ps(PB[3]), func=AF.Ln, bias=cc("eps", 0, 0), scale=1.0),
                  reads=[r_pb[3], r_eps], writes=rA)
            S.add("act", lambda e: e.activation(out=rstd[:, :], in_=rt[:, :], func=AF.Exp, scale=-0.5), reads=rA, writes=[r_tB])
            if nxt is not None:
                act_preload(nxt)

        def rms_apply(gname, l):
            for j in range(KC):
                S.add("dve", lambda e, j=j: e.scalar_tensor_tensor(out=hb[:, j, :], in0=xT[:, j, :], scalar=cc(gname, l, j),
                                                                   in1=rstd[:, :], op0=ALU.mult, op1=ALU.mult),
                      reads=[r_x[j], r_tB, r_cst], writes=[r_h[j]])

        def sview(ap, e_):
            return ap.rearrange("p (s e) -> p s e", e=e_)

        def pre_hist_ops(l, pair):
            ops = []
            items = []
            for n_, j in enumerate(pair):
                ub = n_ % 2
                i = hsc["i"] % NDG
                hsc["i"] += 1
                items.append((j, ub, i))
                ops.append(("sp", lambda e, j=j, ub=ub: e.dma_start(out=ubs[ub][:, :], in_=sb_in[l, j, :, :]),
                            dict(writes=[r_ubs[ub]], chan=ch_ubs_l[ub])))
            for k in range(WB - 1):
                ne = min(DEC, WB - 1 - k)
                for (j, ub, i) in items:
                    wk = cc("wb", l, j * WB + k)
                    o_ap = dg[i][:, 0:ne * NS]
                    i_ap = ubs[ub][:, k * NS:(k + ne) * NS]
                    if k == 0:
                        ops.append(("dve", lambda e, o_ap=o_ap, i_ap=i_ap, wk=wk: e.tensor_scalar_mul(out=o_ap, in0=i_ap, scalar1=wk),
                                    dict(reads=[r_ubs[ub], r_cst], writes=[r_dg[i]], relax=True)))
                    else:
                        ops.append(("dve", lambda e, o_ap=o_ap, i_ap=i_ap, wk=wk: e.scalar_tensor_tensor(
                            out=o_ap, in0=i_ap, scalar=wk, in1=o_ap, op0=ALU.mult, op1=ALU.add),
                            dict(reads=[r_ubs[ub], r_cst, r_dg[i]], writes=[r_dg[i]], relax=True)))
            for (j, ub, i) in items:
                ops.append(("sp", lambda e, j=j, i=i: e.dma_start(out=hs[l, j, :, :], in_=dg[i][:, :]),
                            dict(reads=[r_dg[i]], writes=[r_hs[l][j]], chan=ch_hs_s[i])))
            return ops

        hsc = {"i": 0}

        def flush_ops(ops):
            for (eng, fn, kw) in ops:
                S.add(eng, fn, **kw)

        for ti, (TP, TS) in enumerate(cfg.tiles):
            c0 = ti * T
            NDVE = cfg.NDVE[ti]
            segs = []
            for hh in range(2):
                a_, b_ = hh * H, (hh + 1) * H
                if min(b_, TP) > a_:
                    segs.append(("p", hh, 0, min(b_, TP) - a_, a_))
                if b_ > max(a_, TP):
                    sa0 = max(a_, TP)
                    segs.append(("s", hh, sa0 - a_, b_ - a_, sa0 - TP))
            rA_used = rA if TS else rA[0:1]
            rAcc_used = rAcc if TS else rAcc[0:1]

            if ti == 0:
                for k in range(KC):
                    S.add("sp", lambda e, c0=c0, k=k: e.dma_start(out=xT[:, k, :], in_=xin[k * 128:(k + 1) * 128, c0:c0 + T]),
                          writes=[r_x[k]], chan=ch_x[k])
                while ws["issued"] < cfg.NSLOT:
                    ws_issue(after=[r_x[KC - 1]])
            pbrot["n"] = 3
            for j in range(KC):
                stats_step(j)

            for l in range(NL):
                rms_finish(AF.Sigmoid)
                rms_apply("g1", l)

                NPEt = cfg.NPE[ti]
                pbrot["n"] = 3 if NPEt else 4

                def run_a1(l, TP, TS, segs, rAcc_used):
                    def uq(ap):
                        return ap.bitcast(F32R) if NPEt else ap
                    KD = WB - NPEt

                    def a1_front(j):
                        ub = j % 2
                        A, Um = [], []
                        s, ncols, kcm = ws_acquire("A1a", j)
                        p1 = 0 if NPEt else next_pb()
                        mm(p1, s, ncols, kcm, 0, h_in, r_h, fine=(j == 0))
                        S.add("act", lambda e, p1=p1: e.activation(out=v2(tmpA[:, :]), in_=ps(PB[p1]), func=AF.Copy),
                              reads=[r_pb[p1]], writes=rA)
                        S.add("act", lambda e: e.activation(out=cvp[:, 0:2], in_=carry_a[:, (l * KC + j) * 2:(l * KC + j) * 2 + 2], func=AF.Copy),
                              reads=[r_ca], writes=[r_cvp])
                        if TS:
                            S.add("sp", lambda e: e.dma_start(out=cvs[:, :], in_=sa_in[l, j, :, :]), writes=[r_cvs], chan=ch_cvs_l)
                        p2 = 1 if NPEt else next_pb()
                        mm(p2, s, ncols, kcm, 1, h_in, r_h)
                        ws_release()
                        if not TS:
                            A.append(("dve", lambda e: e.tensor_tensor(out=v2(cvp[:, 2:2 + T]), in0=ps(PB[p2]), in1=v2(tmpA[:, :]), op=ALU.mult),
                                      dict(reads=[r_pb[p2]] + rA, writes=[r_cvp], relax=True)))
                        for (kind, hh, a0, a1, do) in (segs if TS else []):
                            n = a1 - a0
                            if kind == "p":
                                A.append(("dve", lambda e, hh=hh, a0=a0, a1=a1, do=do, n=n: e.tensor_tensor(
                                    out=cvp[:, 2 + do:2 + do + n], in0=PB[p2][:, hh, a0:a1], in1=tmpA[:, hh * H + a0:hh * H + a1], op=ALU.mult),
                                    dict(reads=[r_pb[p2]] + rA, writes=[r_cvp], relax=True)))
                            else:
                                A.append(("dve", lambda e, hh=hh, a0=a0, a1=a1, do=do, n=n: e.tensor_tensor(
                                    out=cvs[:, 2 * NS + do:2 * NS + do + n], in0=PB[p2][:, hh, a0:a1],
                                    in1=tmpA[:, hh * H + a0:hh * H + a1], op=ALU.mult),
                                    dict(reads=[r_pb[p2]] + rA, writes=[r_cvs], relax=True)))
                        for k in range(WA):
                            wk = cc("wa", l, j * WA + k)
                            if k == 0:
                                A.append(("dve", lambda e, wk=wk: e.tensor_scalar_mul(out=tmpA[:, 0:TP], in0=cvp[:, 0:TP], scalar1=wk),
                                          dict(reads=[r_cvp, r_cst], writes=[rA[0]], relax=True)))
                                if TS:
                                    A.append(("dve", lambda e, wk=wk: e.tensor_scalar_mul(
                                        out=tmpA[:, TP:T], in0=cvs[:, 0:DEC * NS], scalar1=wk),
                                        dict(reads=[r_cvs, r_cst], writes=[rA[1]], relax=True)))
                            else:
                                A.append(("dve", lambda e, wk=wk, k=k: e.scalar_tensor_tensor(
                                    out=tmpA[:, 0:TP], in0=cvp[:, k:k + TP], scalar=wk, in1=tmpA[:, 0:TP], op0=ALU.mult, op1=ALU.add),
                                    dict(reads=[r_cvp, r_cst, rA[0]], writes=[rA[0]], relax=True)))
                                if TS:
                                    A.append(("dve", lambda e, wk=wk, k=k: e.scalar_tensor_tensor(
                                        out=tmpA[:, TP:T], in0=cvs[:, k * NS:(k + DEC) * NS], scalar=wk,
                                        in1=tmpA[:, TP:T], op0=ALU.mult, op1=ALU.add),
                                        dict(reads=[r_cvs, r_cst, rA[1]], writes=[rA[1]], relax=True)))
                        s, ncols, kcm = ws_acquire("A1b", j)
                        p3 = 0 if NPEt else next_pb()
                        mm(p3, s, ncols, kcm, 0, h_in, r_h)
                        if NPEt:
                            S.add("act", lambda e: e.activation(out=v2(tmpBa[:, :]), in_=ps(PB[p3]), func=AF.Copy),
                                  reads=[r_pb[p3]], writes=r_sq)
                            A.append(("dve", lambda e: e.tensor_tensor(out=blk(j), in0=tmpBa[:, :], in1=tmpA[:, :], op=ALU.mult),
                                      dict(reads=r_sq + rA, writes=[r_u[j]], relax=True)))
                        else:
                            A.append(("dve", lambda e: e.tensor_tensor(out=v2(blk(j)), in0=ps(PB[p3]), in1=v2(tmpA[:, :]), op=ALU.mult),
                                      dict(reads=[r_pb[p3]] + rA, writes=[r_u[j]], relax=True)))
                        p5 = 2 if NPEt else next_pb()
                        mm(p5, s, ncols, kcm, 1, h_in, r_h)
                        ws_release()
                        if TS:
                            S.add("sp", lambda e: e.dma_start(out=ubs[ub][:, :], in_=sb_in[l, j, :, :]), writes=[r_ubs[ub]], chan=ch_ubs_l[ub])
                        s, ncols, kcm = ws_acquire("A1c", j)
                        p4 = 0 if NPEt else next_pb()
                        mm(p4, s, ncols, kcm, 0, h_in, r_h)
                        ws_release()
                        S.add("act", lambda e: e.activation(out=v2(tmpB[:, :]), in_=ps(PB[p4]), func=AF.Sigmoid),
                              reads=[r_pb[p4]], writes=[r_tB])
                        S.add("act", lambda e: e.activation(out=uq(ubp[ub][:, 0:30]), in_=carry_b[:, (l * KC + j) * 30:(l * KC + j) * 30 + 30], func=AF.Copy),
                              reads=[r_cb_], writes=[r_ubp[ub]])
                        if not TS:
                            Um.append(("dve", lambda e: e.tensor_tensor(out=v2(uq(ubp[ub][:, 30:30 + T])), in0=ps(PB[p5]), in1=v2(tmpB[:, :]), op=ALU.mult),
                                       dict(reads=[r_pb[p5], r_tB], writes=[r_ubp[ub]], relax=True)))
                        for (kind, hh, a0, a1, do) in (segs if TS else []):
                            n = a1 - a0
                            if kind == "p":
                                Um.append(("dve", lambda e, hh=hh, a0=a0, a1=a1, do=do, n=n: e.tensor_tensor(
                                    out=uq(ubp[ub][:, 30 + do:30 + do + n]), in0=PB[p5][:, hh, a0:a1], in1=tmpB[:, hh * H + a0:hh * H + a1], op=ALU.mult),
                                    dict(reads=[r_pb[p5], r_tB], writes=[r_ubp[ub]], relax=True)))
                            else:
                                Um.append(("dve", lambda e, hh=hh, a0=a0, a1=a1, do=do, n=n: e.tensor_tensor(
                                    out=ubs[ub][:, 30 * NS + do:30 * NS + do + n], in0=PB[p5][:, hh, a0:a1],
                                    in1=tmpB[:, hh * H + a0:hh * H + a1], op=ALU.mult),
                                    dict(reads=[r_pb[p5], r_tB], writes=[r_ubs[ub]], relax=True)))
                        return A, Um

                    def a1_ca_tail(j):
                        S.add("act", lambda e: e.activation(out=carry_a[:, (l * KC + j) * 2:(l * KC + j) * 2 + 2], in_=cvp[:, TP:TP + 2], func=AF.Copy),
                              reads=[r_cvp], writes=[r_ca])
                        if TS:
                            S.add("sp", lambda e: e.dma_start(out=ncas[l, j, :, :], in_=cvs[:, :]), reads=[r_cvs], chan=ch_cvs_s)

                    def a1_taps(j):
                        ub = j % 2
                        Y = []
                        for k in range(WB):
                            wk = cc("wb", l, j * WB + k)
                            dst, rdst = (acc, rAcc[0]) if k % 2 == 0 else (accB, rAccB)
                            o_ap, i_ap = dst[:, 0:TP], ubp[ub][:, k:k + TP]
                            if k >= KD:
                                pass
                            elif k < 2:
                                Y.append(("dve", lambda e, o_ap=o_ap, i_ap=i_ap, wk=wk: e.tensor_scalar_mul(out=o_ap, in0=i_ap, scalar1=wk),
                                          dict(reads=[r_ubp[ub], r_cst], writes=[rdst], relax=True)))
                            else:
                                Y.append(("dve", lambda e, o_ap=o_ap, i_ap=i_ap, wk=wk: e.scalar_tensor_tensor(
                                    out=o_ap, in0=i_ap, scalar=wk, in1=o_ap, op0=ALU.mult, op1=ALU.add),
                                    dict(reads=[r_ubp[ub], r_cst, rdst], writes=[rdst], relax=True)))
                            if TS and k == 0:
                                Y.append(("sp", lambda e: e.dma_start(out=acc[:, TP:T], in_=hs[l, j, :, :]),
                                          dict(reads=[r_hs[l][j]], writes=[rAcc[1]], chan=ch_hl)))
                            if TS and k >= WB - DEC:
                                e0 = WB - 1 - k
                                o_ap = acc[:, TP + e0 * NS:T]
                                i_ap = ubs[ub][:, (WB - 1) * NS:(WB - 1 + DEC - e0) * NS]
                                Y.append(("dve", lambda e, o_ap=o_ap, i_ap=i_ap, wk=wk: e.scalar_tensor_tensor(
                                    out=o_ap, in0=i_ap, scalar=wk, in1=o_ap, op0=ALU.mult, op1=ALU.add),
                                    dict(reads=[r_ubs[ub], r_cst, rAcc[1]], writes=[rAcc[1]], relax=True)))
                        return Y

                    def a1_pe_plan(j):
                        idx = []
                        for _k in range(KD, WB):
                            idx.append(dgc["i"] % NDG)
                            dgc["i"] += 1
                        return idx

                    def a1_pe_diag(j, idx, t0, t1):
                        for t in range(t0, t1):
                            k = KD + t
                            wk = cc("wb", l, j * WB + k)
                            S.add("act", lambda e, i=idx[t], wk=wk: e.activation(out=dg[i][:, :].bitcast(F32R), in_=cc("ident", 0, 0, 128), func=AF.Copy, scale=wk),
                                  reads=[r_cst], writes=[r_dg[idx[t]]])

                    def a1_pe_taps(j, idx, npre):
                        ub = j % 2
                        for t in range(WB - KD):
                            k = KD + t
                            i = idx[t]
                            if t >= npre:
                                a1_pe_diag(j, idx, t, t + 1)

                            def fn(e, i=i, k=k):
                                li = None
                                lhsT = dg[i][:, :].bitcast(F32R)
                                if TP == T:
                                    for hh in range(2):
                                        li = e.matmul(PB[3][:, hh, 0:H], lhsT, ubp[ub][:, k + hh * H:k + (hh + 1) * H].bitcast(F32R),
                                                      start=(k == KD), stop=(k == WB - 1))
                                else:
                                    li = e.matmul(PB[3][:, 0, 0:TP], lhsT, ubp[ub][:, k:k + TP].bitcast(F32R),
                                                  start=(k == KD), stop=(k == WB - 1))
                                return li
                            S.add("pe", fn, reads=[r_dg[i], r_ubp[ub]], writes=[r_pb[3]])

                    def a1_back_tail(j):
                        ub = j % 2
                        bj = cc("bb", l, j)
                        S.add("dve", lambda e: e.scalar_tensor_tensor(out=cbf(j)[:, 0:TP], in0=acc[:, 0:TP], scalar=bj, in1=accB[:, 0:TP],
                                                                      op0=ALU.add, op1=ALU.add),
                              reads=[rAcc[0], rAccB, r_cst], writes=r_cbf(j), relax=True)
                        if TS:
                            S.add("dve", lambda e: e.tensor_scalar_add(out=cbf(j)[:, TP:T], in0=acc[:, TP:T], scalar1=bj),
                                  reads=[rAcc[1], r_cst], writes=r_cbf(j), relax=True)
                        if NPEt:
                            if TP == T:
                                S.add("dve", lambda e: e.tensor_tensor(out=v2(cbf(j)), in0=ps(PB[3]), in1=v2(cbf(j)), op=ALU.add),
                                      reads=r_cbf(j) + [r_pb[3]], writes=r_cbf(j))
                            else:
                                S.add("dve", lambda e: e.tensor_tensor(out=cbf(j)[:, 0:TP], in0=PB[3][:, 0, 0:TP], in1=cbf(j)[:, 0:TP], op=ALU.add),
                                      reads=r_cbf(j) + [r_pb[3]], writes=r_cbf(j))
                        S.add("act", lambda e: e.activation(out=carry_b[:, (l * KC + j) * 30:(l * KC + j) * 30 + 30], in_=ubp[ub][:, TP:TP + 30], func=AF.Copy),
                              reads=[r_ubp[ub]], writes=[r_cb_])
                        if TS:
                            S.add("sp", lambda e: e.dma_start(out=ncbs[l, j, :, :], in_=ubs[ub][:, :]), reads=[r_ubs[ub]], chan=ch_ubs_s[ub])

                    def flush(ops):
                        for (eng, fn, kw) in ops:
                            S.add(eng, fn, **kw)

                    prevY = None
                    dgc = {"i": 0}
                    for j in range(KC + 1):
                        if j >= 1 and NPEt:
                            pidx = a1_pe_plan(j - 1)
                            npre = min(NPEt, NDG)
                            a1_pe_diag(j - 1, pidx, 0, npre)
                        A, Um = a1_front(j) if j < KC else ([], [])
                        if j >= 1 and NPEt:
                            a1_pe_taps(j - 1, pidx, npre)
                        Y = prevY if prevY is not None else []
                        X = A + Um
                        nY, nX = len(Y), len(X)
                        start = 0
                        merged, xi = [], 0
                        for yi, y in enumerate(Y):
                            merged.append(y)
                            if yi >= start and (yi - start) % 2 == 1 and xi < nX:
                                merged.append(X[xi])
                                xi += 1
                        merged.extend(X[xi:])
                        flush(merged)
                        if j < KC:
                            a1_ca_tail(j)
                        if j >= 1:
                            a1_back_tail(j - 1)
                        prevY = a1_taps(j) if j < KC else None


                run_a1(l, TP, TS, segs, rAcc_used)
                act_preload(AF.Ln)

                for j in range(KC):
                    sj = j % 2
                    S.add("act", lambda e, j=j, sj=sj: e.activation(out=sq[sj], in_=cbf(j), func=AF.Square),
                          reads=r_cbf(j), writes=[r_sq[sj]])

                    def fn(e, j=j, sj=sj):
                        li = None
                        for hh in range(2):
                            e.matmul(PB[2][:, hh, 0:H], ones_f[:, :], cbf(j)[:, hh * H:(hh + 1) * H], start=(j == 0), stop=(j == KC - 1))
                            li = e.matmul(PB[3][:, hh, 0:H], ones_bf[:, :], sq[sj][:, hh * H:(hh + 1) * H], start=(j == 0), stop=(j == KC - 1))
                        return li
                    S.add("pe", fn, reads=[r_sq[sj], r_ones] + r_cbf(j), writes=[r_pb[2], r_pb[3]])

                S.add("act", lambda e: e.activation(out=v2(meanb[:, :]), in_=ps(PB[2]), func=AF.Copy), reads=[r_pb[2]], writes=rAcc)
                S.add("act", lambda e: e.activation(out=v2(rt[:, :]), in_=ps(PB[2]), func=AF.Square), reads=[r_pb[2]], writes=rA)
                S.add("dve", lambda e: e.tensor_tensor(out=v2(rt[:, :]), in0=ps(PB[3]), in1=v2(rt[:, :]), op=ALU.subtract), reads=[r_pb[3]] + rA, writes=rA)
                S.add("dve", lambda e: e.tensor_scalar_max(out=rt[:, :], in0=rt[:, :], scalar1=0.0), reads=rA, writes=rA)
                S.add("act", lambda e: e.activation(out=rt[:, :], in_=rt[:, :], func=AF.Ln, bias=cc("eps", 0, 1), scale=1.0), reads=rA + [r_eps], writes=rA)
                S.add("act", lambda e: e.activation(out=rstd[:, :], in_=rt[:, :], func=AF.Exp, scale=-0.5), reads=rA, writes=[r_tB])
                act_preload(AF.Silu)
                for j in range(KC):
                    S.add("dve", lambda e, j=j: e.tensor_tensor(out=cbf(j), in0=cbf(j), in1=meanb[:, :], op=ALU.subtract),
                          reads=r_cbf(j) + rAcc, writes=r_cbf(j))
                    S.add("dve", lambda e, j=j: e.tensor_tensor(out=cbf(j), in0=cbf(j), in1=rstd[:, :], op=ALU.mult),
                          reads=r_cbf(j) + [r_tB], writes=r_cbf(j))
                    S.add("act", lambda e, j=j, l=l: e.activation(out=blk(KC + j), in_=cbf(j), func=AF.Silu, bias=cc("lb", l, j), scale=cc("lg", l, j)),
                          reads=r_cbf(j) + [r_cst], writes=[r_u[KC + j]])

                pbrot["n"] = 4
                ga_res = r_u[0:KC]
                z_res = r_u[KC:2 * KC]
                for j in range(KC):
                    s, ncols, kcm = ws_acquire("A3a", j)
                    p1 = next_pb()
                    mm(p1, s, ncols, kcm, 0, h_in, r_h)
                    S.add("act", lambda e, p1=p1: e.activation(out=v2(tmpA[:, :]), in_=ps(PB[p1]), func=AF.Sigmoid), reads=[r_pb[p1]], writes=rA)
                    p2 = next_pb()
                    mm(p2, s, ncols, kcm, 1, blk_in(0), ga_res)
                    ws_release()
                    S.add("dve", lambda e, p2=p2: e.tensor_tensor(out=v2(tmpA[:, :]), in0=ps(PB[p2]), in1=v2(tmpA[:, :]), op=ALU.mult),
                          reads=[r_pb[p2]] + rA, writes=rA)
                    s, ncols, kcm = ws_acquire("A3b", j)
                    p3 = next_pb()
                    mm(p3, s, ncols, kcm, 0, h_in, r_h)
                    S.add("act", lambda e, p3=p3: e.activation(out=v2(tmpB[:, :]), in_=ps(PB[p3]), func=AF.Sigmoid), reads=[r_pb[p3]], writes=[r_tB])
                    p4 = next_pb()
                    mm(p4, s, ncols, kcm, 1, blk_in(KC), z_res, fine=(j == 0))
                    ws_release()
                    S.add("dve", lambda e, p4=p4: e.tensor_tensor(out=v2(tmpB[:, :]), in0=ps(PB[p4]), in1=v2(tmpB[:, :]), op=ALU.mult),
                          reads=[r_pb[p4], r_tB], writes=[r_tB])
                    S.add("dve", lambda e, j=j: e.tensor_tensor(out=blk(2 * KC + j), in0=tmpA[:, :], in1=tmpB[:, :], op=ALU.add),
                          reads=rA + [r_tB], writes=[r_u[2 * KC + j]])

                act_preload(AF.Ln)
                pbrot["n"] = 3
                m_res = r_u[2 * KC:3 * KC]
                for q in range(KC // cfg.A4G):
                    s, ncols, kcm = ws_acquire("A4", q)
                    for jj in range(cfg.A4G):
                        j = q * cfg.A4G + jj
                        p1 = next_pb()
                        mm(p1, s, ncols, kcm, jj, blk_in(2 * KC), m_res)
                        S.add("dve", lambda e, p1=p1, j=j: e.tensor_tensor(out=v2(xT[:, j, :]), in0=ps(PB[p1]), in1=v2(xT[:, j, :]), op=ALU.add),
                              reads=[r_pb[p1], r_x[j]], writes=[r_x[j]])
                        if j >= 2:
                            stats_step(j - 2)
                    ws_release()
                for j in range(max(KC - 2, 0), KC):
                    stats_step(j)

                rms_finish(AF.Silu)
                rms_apply("g2", l)

                pre = []
                if ti == 0 and cfg.tiles[-1][1]:
                    for j0 in range(0, KC, 2):
                        pre += pre_hist_ops(l, [j0, j0 + 1])
                npre = -(-len(pre) // FC) if pre else 0
                for f in range(FC):
                    s, ncols, kcm = ws_acquire("F1", f)
                    p1 = next_pb()
                    mm(p1, s, ncols, kcm, 0, h_in, r_h, fine=(f == 0))
                    S.add("act", lambda e, p1=p1: e.activation(out=v2(tmpA[:, :]), in_=ps(PB[p1]), func=AF.Silu), reads=[r_pb[p1]], writes=rA)
                    p2 = next_pb()
                    mm(p2, s, ncols, kcm, 1, h_in, r_h)
                    ws_release()
                    S.add("dve", lambda e, p2=p2, f=f: e.tensor_tensor(out=v2(blk(f)), in0=ps(PB[p2]), in1=v2(tmpA[:, :]), op=ALU.mult),
                          reads=[r_pb[p2]] + rA, writes=[r_u[f]])
                    if pre:
                        flush_ops(pre[f * npre:(f + 1) * npre])

                act_preload(AF.Ln)
                ff_res = r_u[0:FC]
                for j in range(KC):
                    p1 = next_pb()
                    for hf in range(2):
                        s, ncols, kcm = ws_acquire("F2", 2 * j + hf)
                        mm(p1, s, ncols, kcm, 0, blk_in(0), ff_res, kc0=hf * cfg.FCH, first=(hf == 0), last=(hf == 1))
                        ws_release()
                    S.add("dve", lambda e, p1=p1, j=j: e.tensor_tensor(out=v2(xT[:, j, :]), in0=ps(PB[p1]), in1=v2(xT[:, j, :]), op=ALU.add),
                          reads=[r_pb[p1], r_x[j]], writes=[r_x[j]])
                    if j >= 2:
                        stats_step(j - 2)
                for j in range(max(KC - 2, 0), KC):
                    stats_step(j)

            rms_finish()
            for j in range(KC):
                S.add("dve", lambda e, j=j: e.scalar_tensor_tensor(out=yv(j), in0=xT[:, j, :], scalar=cc("gf", 0, j), in1=rstd[:, :],
                                                                   op0=ALU.mult, op1=ALU.mult),
                      reads=[r_x[j], r_tB, r_cst], writes=[r_u[2 * j], r_u[2 * j + 1]])
                S.add("sp", lambda e, c0=c0, j=j: e.dma_start(out=yout[j * 128:(j + 1) * 128, c0:c0 + T], in_=U[:, j * T:(j + 1) * T]),
                      reads=[r_u[2 * j], r_u[2 * j + 1]], chan=ch_y[j])
                if ti + 1 < len(cfg.tiles):
                    S.add("sp", lambda e, c1=c0 + T, j=j: e.dma_start(out=xT[:, j, :], in_=xin[j * 128:(j + 1) * 128, c1:c1 + T]),
                          writes=[r_x[j]], chan=ch_x[j])

        S.add("sp", lambda e: e.dma_start(out=ncap[:, :], in_=carry_a[:, :]), reads=[r_ca], chan=ch_fin)
        S.add("sp", lambda e: e.dma_start(out=ncbp[:, :], in_=carry_b[:, :]), reads=[r_cb_], chan=ch_fin)

        S.finalize()

        sem_names = ["pe", "act", "dve", "pool", "sp"]
        esem = {n: es.enter_context(nc.semaphore(f"s_{n}")) for n in sem_names}
        for c in S.chans:
            c.sem = es.enter_context(nc.semaphore(f"c_{c.name}"))
        final_waits = [(c.sem, 16 * c.n) for c in (ch_y + [ch_fin, ch_cvs_s, ch_ubs_s[0], ch_ubs_s[1]]) if c.n > 0]
        block = es.enter_context(nc.Block())

        @block.tensor
        def _(e):
            S.emit("pe", e, esem)

        @block.scalar
        def _(e):
            S.emit("act", e, esem)

        @block.vector
        def _(e):
            S.emit("dve", e, esem)

        @block.gpsimd
        def _(e):
            S.emit("pool", e, esem)

        @block.sync
        def _(e):
            S.emit("sp", e, esem, final_waits=final_waits)

    return nc


def _fm(v, KC):
    return np.ascontiguousarray(v.reshape(KC, 128).T)


def _layout_weights(cfg, w_in, w_out_a, w_out_b, w_o, w_gate, w_up, w_down):
    KC, D = cfg.KC, cfg.D
    out = np.empty((NL, 128, cfg.WTOT), np.float32)
    ar = np.arange(128)
    for l in range(NL):
        for gi, (kind, idx, ncols, kcm) in enumerate(cfg.groups):
            W = w_in[l]
            if kind == "A1a":
                j = idx
                blkm = np.concatenate([W[:, D + j * 128 + ar], W[:, 2 * D + j * 128 + ar]], axis=1)
            elif kind == "A1b":
                j = idx
                blkm = np.concatenate([W[:, j * 128 + ar], W[:, 3 * D + j * 128 + ar]], axis=1)
            elif kind == "A1c":
                j = idx
                blkm = W[:, 4 * D + j * 128 + ar]
            elif kind == "A3a":
                j = idx
                blkm = np.concatenate([W[:, 5 * D + j * 128 + ar], w_out_a[l][:, j * 128 + ar]], axis=1)
            elif kind == "A3b":
                j = idx
                blkm = np.concatenate([W[:, 6 * D + j * 128 + ar], w_out_b[l][:, j * 128 + ar]], axis=1)
            elif kind == "A4":
                blkm = w_o[l][:, idx * ncols:(idx + 1) * ncols]
            elif kind == "F1":
                blkm = np.concatenate([w_gate[l][:, idx * 128 + ar], w_up[l][:, idx * 128 + ar]], axis=1)
            else:
                j, hf = idx // 2, idx % 2
                blkm = w_down[l][hf * kcm * 128:(hf + 1) * kcm * 128, j * 128:(j + 1) * 128]
            assert blkm.shape == (kcm * 128, ncols), (blkm.shape, kind)
            t = blkm.reshape(kcm, 128, ncols).transpose(1, 0, 2).reshape(128, kcm * ncols)
            o = cfg.goffs[gi]
            out[l, :, o:o + kcm * ncols] = t
    return out


def _layout_consts(cfg, norm_mix_g, conv_a_w, conv_b_w, conv_b_bias, ln_b_g, ln_b_b, norm_ffn_g, final_norm_g):
    KC = cfg.KC
    c = np.zeros((128, cfg.NCONST), np.float32)
    for l in range(NL):
        c[:, cfg.coff[("g1", l)]:][:, :KC] = _fm(norm_mix_g[l], KC)
        wa = conv_a_w[l].reshape(WA, KC, 128).transpose(2, 1, 0).reshape(128, KC * WA)
        c[:, cfg.coff[("wa", l)]:][:, :KC * WA] = wa
        wb = conv_b_w[l].reshape(WB, KC, 128).transpose(2, 1, 0).reshape(128, KC * WB)
        c[:, cfg.coff[("wb", l)]:][:, :KC * WB] = wb
        c[:, cfg.coff[("bb", l)]:][:, :KC] = _fm(conv_b_bias[l], KC)
        c[:, cfg.coff[("lg", l)]:][:, :KC] = _fm(ln_b_g[l], KC)
        c[:, cfg.coff[("lb", l)]:][:, :KC] = _fm(ln_b_b[l], KC)
        c[:, cfg.coff[("g2", l)]:][:, :KC] = _fm(norm_ffn_g[l], KC)
    c[:, cfg.coff[("gf", 0)]:][:, :KC] = _fm(final_norm_g, KC)
    c[:, cfg.coff[("ident", 0)]:][:, :128] = np.eye(128, dtype=np.float32)
    return c


def run(cfg, x_prompt, x_sample, state_conv_a, state_conv_b, norm_mix_g, w_in, conv_a_w, w_out_a,
        conv_b_w, conv_b_bias, ln_b_g, ln_b_b, w_out_b, w_o, norm_ffn_g, w_gate, w_up, w_down,
        final_norm_g, trace=False):
    f = lambda a: np.asarray(a, dtype=np.float32)
    x_prompt, x_sample, state_conv_a, state_conv_b = f(x_prompt), f(x_sample), f(state_conv_a), f(state_conv_b)
    KC, D, NS, NPC, SEQ = cfg.KC, cfg.D, cfg.NS, cfg.NPC, cfg.SEQ
    wts = _layout_weights(cfg, f(w_in), f(w_out_a), f(w_out_b), f(w_o), f(w_gate), f(w_up), f(w_down))
    cst = _layout_consts(cfg, f(norm_mix_g), f(conv_a_w), f(conv_b_w), f(conv_b_bias), f(ln_b_g), f(ln_b_b),
                         f(norm_ffn_g), f(final_norm_g))
    in_maps = []
    for c in range(NCORES):
        b, hf = c // 2, c % 2
        st = 0 if hf == 0 else SEQ - NPC
        xs = x_sample[NS * c:NS * (c + 1)].transpose(1, 0, 2).reshape(DEC * NS, D)
        xc = np.concatenate([x_prompt[b, st:st + NPC], xs], axis=0)
        xin = np.ascontiguousarray(xc.T)
        sa = np.zeros((NL, KC, 128, 10, NS), np.float32)
        sa[:, :, :, 0:2, :] = state_conv_a[:, NS * c:NS * (c + 1)].reshape(NL, NS, 2, KC, 128).transpose(0, 3, 4, 2, 1)
        sbb = np.zeros((NL, KC, 128, 38, NS), np.float32)
        sbb[:, :, :, 0:30, :] = state_conv_b[:, NS * c:NS * (c + 1)].reshape(NL, NS, 30, KC, 128).transpose(0, 3, 4, 2, 1)
        in_maps.append({"xin": xin, "sa": sa.reshape(NL, KC, 128, NS * 10), "sb": sbb.reshape(NL, KC, 128, NS * 38),
                        "wts": wts, "cst": cst})
    nc = build(cfg)
    res = run_bass_kernel_spmd(nc, in_maps, core_ids=list(range(NCORES)), **({"trace": True} if trace else {}))
    R = res.results
    y_prompt = np.empty((cfg.BATCH, SEQ, D), np.float32)
    y_sample = np.empty((cfg.NSAMP, DEC, D), np.float32)
    nca_p = np.empty((NL, cfg.BATCH, 2, D), np.float32)
    ncb_p = np.empty((NL, cfg.BATCH, 30, D), np.float32)
    nca_s = np.empty((NL, cfg.NSAMP, 2, D), np.float32)
    ncb_s = np.empty((NL, cfg.NSAMP, 30, D), np.float32)
    for c in range(NCORES):
        b, hf = c // 2, c % 2
        yt = np.asarray(R[c]["yout"]).T
        if hf == 0:
            y_prompt[b, 0:NPC] = yt[0:NPC]
        else:
            y_prompt[b, NPC:SEQ] = yt[cfg.HALO:NPC]
            a = np.asarray(R[c]["ncap"]).reshape(128, NL, KC, 2)
            nca_p[:, b] = a.transpose(1, 3, 2, 0).reshape(NL, 2, D)
            bb = np.asarray(R[c]["ncbp"]).reshape(128, NL, KC, 30)
            ncb_p[:, b] = bb.transpose(1, 3, 2, 0).reshape(NL, 30, D)
        y_sample[NS * c:NS * (c + 1)] = yt[NPC:].reshape(DEC, NS, D).transpose(1, 0, 2)
        a = np.asarray(R[c]["ncas"]).reshape(NL, KC, 128, 10, NS)[:, :, :, 8:10, :]
        nca_s[:, NS * c:NS * (c + 1)] = a.transpose(0, 4, 3, 1, 2).reshape(NL, NS, 2, D)
        bb = np.asarray(R[c]["ncbs"]).reshape(NL, KC, 128, 38, NS)[:, :, :, 8:38, :]
        ncb_s[:, NS * c:NS * (c + 1)] = bb.transpose(0, 4, 3, 1, 2).reshape(NL, NS, 30, D)
    outs = (y_prompt, y_sample, nca_p, ncb_p, nca_s, ncb_s)
    if trace:
        return outs, res
    return outs


def kernel(**inputs):
    return run(FULL, **inputs)
```

```python
import contextlib
import numpy as np
import concourse.bass as bass
import concourse.mybir as mybir
from concourse.bass_utils import run_bass_kernel_spmd

F32 = mybir.dt.float32
F32R = mybir.dt.float32r
BF16 = mybir.dt.bfloat16
AF = mybir.ActivationFunctionType
ALU = mybir.AluOpType

RMS_EPS = 1e-6
LN_EPS = 1e-5
WA = 3
WB = 31
DEC = 8
NL = 2
NCORES = 8


class Cfg:
    def __init__(self, D=2048, DFF=5632, SEQ=2048, BATCH=4, NSAMP=128, NPC=1056, T=592, NSLOT=5, NDVE=(26, 26), NPE=(0, 0)):
        self.D, self.DFF, self.SEQ, self.BATCH, self.NSAMP = D, DFF, SEQ, BATCH, NSAMP
        self.KC, self.FC = D // 128, DFF // 128
        self.NPC, self.T, self.H = NPC, T, T // 2
        self.NS = NSAMP // NCORES
        self.TSAMP = self.NS * DEC
        self.NCOLS = NPC + self.TSAMP
        assert self.NCOLS == 2 * T and T % 2 == 0 and self.H <= 512
        self.tiles = [(T, 0), (NPC - T, self.TSAMP)]
        assert NPC - T >= WB - 1
        self.HALO = 2 * NPC - SEQ
        assert self.HALO >= 2 * (WB - 1)
        self.NSLOT = NSLOT
        self.NDVE = NDVE
        self.NPE = NPE
        KC, FC = self.KC, self.FC
        groups = []
        self.FCH = FC // 2
        for j in range(KC):
            groups.append(("A1a", j, 256, KC))
            groups.append(("A1b", j, 256, KC))
            groups.append(("A1c", j, 128, KC))
        for j in range(KC):
            groups.append(("A3a", j, 256, KC))
            groups.append(("A3b", j, 256, KC))
        self.A4G = min(2, KC)
        for q in range(KC // self.A4G):
            groups.append(("A4", q, 128 * self.A4G, KC))
        for f in range(FC):
            groups.append(("F1", f, 256, KC))
        for j in range(KC):
            groups.append(("F2", 2 * j, 128, self.FCH))
            groups.append(("F2", 2 * j + 1, 128, self.FCH))
        self.groups = groups
        offs, o = [], 0
        for (_, _, nc_, kcm) in groups:
            offs.append(o)
            o += nc_ * kcm
        self.goffs, self.WTOT = offs, o
        self.SLOT_L = max(nc_ * kcm for (_, _, nc_, kcm) in groups)
        c = {}
        o = 0
        for l in range(NL):
            for nm, n in (("g1", KC), ("wa", KC * WA), ("wb", KC * WB), ("bb", KC), ("lg", KC), ("lb", KC), ("g2", KC)):
                c[(nm, l)] = o
                o += n
        c[("gf", 0)] = o
        o += KC
        c[("ident", 0)] = o
        o += 128
        self.coff, self.NCONST = c, o


FULL = Cfg()


class Res:
    __slots__ = ("name", "w", "r")

    def __init__(self, name):
        self.name, self.w, self.r = name, None, []


class Chan:
    __slots__ = ("name", "n", "sem")

    def __init__(self, name):
        self.name, self.n, self.sem = name, 0, None


class Op:
    __slots__ = ("eng", "fn", "deps", "chan", "cnt", "need", "relax", "idx")

    def __init__(self, eng, fn, chan):
        self.eng, self.fn, self.chan = eng, fn, chan
        self.deps, self.cnt, self.need = (), None, False
        self.relax, self.idx = False, 0


def _skip_dep(op, d):
    if d.chan is not None or d.eng != op.eng:
        return False
    if op.eng == "pe":
        return True
    return op.relax and d.relax and (op.idx - d.idx) >= 2


ENGS = ("pe", "act", "dve", "pool", "sp")


class Sched:
    def __init__(self):
        self.ops = {e: [] for e in ENGS}
        self.chans = []

    def chan(self, name):
        c = Chan(name)
        self.chans.append(c)
        return c

    def add(self, eng, fn, reads=(), writes=(), chan=None, ndma=1, relax=False):
        op = Op(eng, fn, chan)
        op.relax, op.idx = relax, len(self.ops[eng])
        deps = set()
        for r in reads:
            if r.w is not None:
                deps.add(r.w)
        for w in writes:
            if w.w is not None:
                deps.add(w.w)
            deps.update(w.r)
        for r in reads:
            r.r.append(op)
        for w in writes:
            w.w = op
            w.r = []
        deps.discard(op)
        op.deps = deps
        if chan is not None:
            chan.n += ndma
            op.cnt = 16 * chan.n
            op.need = True
        self.ops[eng].append(op)
        return op

    def finalize(self):
        for e in ENGS:
            for op in self.ops[e]:
                for d in op.deps:
                    if _skip_dep(op, d):
                        continue
                    d.need = True
        for e in ENGS:
            c = 0
            for op in self.ops[e]:
                if op.chan is None and op.need:
                    c += 1
                    op.cnt = c

    def emit(self, eng, e, esem, final_waits=()):
        waited = {}
        for op in self.ops[eng]:
            w = {}
            for d in op.deps:
                if _skip_dep(op, d):
                    continue
                if d.chan is not None:
                    key, sem = ("c", id(d.chan)), d.chan.sem
                else:
                    key, sem = ("e", d.eng), esem[d.eng]
                if d.cnt > w.get(key, (None, 0))[1]:
                    w[key] = (sem, d.cnt)
            for key, (sem, v) in w.items():
                if waited.get(key, 0) < v:
                    e.wait_ge(sem, v)
                    waited[key] = v
            inst = op.fn(e)
            if op.chan is not None:
                for ii in (inst if isinstance(inst, list) else [inst]):
                    ii.then_inc(op.chan.sem, 16)
            elif op.need:
                inst.then_inc(esem[eng], 1)
        for (sem, v) in final_waits:
            e.wait_ge(sem, v)


def build(cfg):
    KC, FC, T, H, NS = cfg.KC, cfg.FC, cfg.T, cfg.H, cfg.NS
    D = cfg.D
    nc = bass.Bass("TRN2", target_bir_lowering=False)
    S = Sched()

    def dram(name, shape, kind):
        return nc.dram_tensor(name, shape, F32, kind=kind).ap()

    xin = dram("xin", [D, cfg.NCOLS], "ExternalInput")
    sa_in = dram("sa", [NL, KC, 128, NS * 10], "ExternalInput")
    sb_in = dram("sb", [NL, KC, 128, NS * 38], "ExternalInput")
    wts = dram("wts", [NL, 128, cfg.WTOT], "ExternalInput")
    cst_in = dram("cst", [128, cfg.NCONST], "ExternalInput")
    yout = dram("yout", [D, cfg.NCOLS], "ExternalOutput")
    ncap = dram("ncap", [128, NL * KC * 2], "ExternalOutput")
    ncbp = dram("ncbp", [128, NL * KC * 30], "ExternalOutput")
    ncas = dram("ncas", [NL, KC, 128, NS * 10], "ExternalOutput")
    ncbs = dram("ncbs", [NL, KC, 128, NS * 38], "ExternalOutput")

    hs = nc.dram_tensor("hs", [NL, KC, 128, NS * DEC], F32, kind="ExternalOutput").ap()
    es = contextlib.ExitStack()
    with es:
        def sb(name, shape, dt=F32):
            return es.enter_context(nc.sbuf_tensor(name, shape, dt))

        if getattr(cfg, "PAD", 0):
            sb("pad", [128, cfg.PAD // 4])
        xT = sb("xT", [128, KC, T])
        hb = sb("hb", [128, KC, T], BF16)
        U = sb("U", [128, 2 * KC * T])
        slots = [sb(f"slot{i}", [128, cfg.SLOT_L], BF16) for i in range(cfg.NSLOT)]
        cst = sb("cst_sb", [128, cfg.NCONST])
        carry_a = sb("carry_a", [128, NL * KC * 2])
        carry_b = sb("carry_b", [128, NL * KC * 30])
        cvp = sb("cvp", [128, 2 + T])
        cvs = sb("cvs", [128, NS * 10])
        ubp = [sb(f"ubp{i}", [128, 30 + T]) for i in range(2)]
        ubs = [sb(f"ubs{i}", [128, NS * 38]) for i in range(2)]
        tmpA = sb("tmpA", [128, T])
        tmpB = sb("tmpB", [128, T])
        acc = sb("acc", [128, T])
        accB = sb("accB", [128, T])
        tmpBa = sb("tmpBa", [128, T])
        _tb = tmpBa[:, :].bitcast(BF16)
        sq = [_tb[:, 0:T], _tb[:, T:2 * T]]
        NDG = 6
        dg = [sb(f"dg{i}", [128, 128]) for i in range(NDG)]
        r_dg = [Res(f"dg{i}") for i in range(NDG)]
        rt, rstd, meanb = tmpA, tmpB, acc
        ones_bf = sb("ones_bf", [128, 128], BF16)
        ones_f = sb("ones_f", [128, 128])
        PB = [es.enter_context(nc.psum_tensor(f"pb{i}", [128, 2, 512], F32)) for i in range(4)]

        r_x = [Res(f"x{j}") for j in range(KC)]
        r_h = [Res(f"h{j}") for j in range(KC)]
        r_u = [Res(f"u{b}") for b in range(4 * KC)]
        r_slot = [Res(f"slot{i}") for i in range(cfg.NSLOT)]
        r_cst = Res("cst")
        r_ca, r_cb_ = Res("carry_a"), Res("carry_b")
        r_cvp, r_cvs = Res("cvp"), Res("cvs")
        r_ubp = [Res("ubp0"), Res("ubp1")]
        r_ubs = [Res("ubs0"), Res("ubs1")]
        rA = [Res("tmpA_p"), Res("tmpA_s")]
        r_tB = Res("tmpB")
        rAcc = [Res("acc_p"), Res("acc_s")]
        rAccB = Res("accB")
        r_sq = [Res("sq0"), Res("sq1")]
        r_ones = Res("ones")
        r_pb = [Res(f"pb{i}") for i in range(4)]

        ch_slot = [S.chan(f"slot{i}") for i in range(cfg.NSLOT)]
        ch_c = S.chan("cst")
        ch_y = [S.chan(f"y{k}") for k in range(KC)]
        ch_x = [S.chan(f"x{k}") for k in range(KC)]
        ch_cvs_l, ch_cvs_s = S.chan("cvs_l"), S.chan("cvs_s")
        ch_ubs_l = [S.chan("ubs_l0"), S.chan("ubs_l1")]
        ch_ubs_s = [S.chan("ubs_s0"), S.chan("ubs_s1")]
        ch_fin = S.chan("fin")
        ch_hs_s = [S.chan(f"hs_s{i}") for i in range(NDG)]
        ch_hl = S.chan("hs_l")
        r_hs = [[Res(f"hs{l}_{j}") for j in range(KC)] for l in range(NL)]
        KH = WB - DEC

        Ubf = U[:, :].bitcast(BF16)

        def blk(b):
            return Ubf[:, b * T:(b + 1) * T]

        def v2(ap):
            return ap.rearrange("p (a n) -> p a n", a=2)

        def ps(pb):
            return pb[:, :, 0:H]

        def cbf(j):
            return U[:, KC * T + j * T: KC * T + (j + 1) * T]

        def r_cbf(j):
            return [r_u[2 * KC + 2 * j], r_u[2 * KC + 2 * j + 1]]

        def yv(j):
            return U[:, j * T:(j + 1) * T]

        eps_t = sb("eps_t", [128, 2])
        r_eps = Res("eps")
        dummy = sb("act_dummy_t", [128, 2])

        def act_preload(func):
            S.add("act", lambda e: e.activation(out=dummy[:, 0:1], in_=eps_t[:, 0:1], func=func), reads=[r_eps])

        def cc(nm, l, i, n=1):
            if nm == "eps":
                return eps_t[:, i:i + 1]
            o = cfg.coff[(nm, l)] + i
            return cst[:, o:o + n]

        S.add("sp", lambda e: e.dma_start(out=cst[:, :], in_=cst_in[:, :]), writes=[r_cst], chan=ch_c)
        S.add("dve", lambda e: e.memset(ones_bf[:, :], 1.0 / D), writes=[r_ones])
        S.add("dve", lambda e: e.memset(ones_f[:, :], 1.0 / D), writes=[r_ones])
        S.add("dve", lambda e: e.memset(carry_a[:, :], 0.0), writes=[r_ca])
        S.add("dve", lambda e: e.memset(carry_b[:, :], 0.0), writes=[r_cb_])
        S.add("dve", lambda e: e.memset(eps_t[:, 0:1], RMS_EPS), writes=[r_eps])
        S.add("dve", lambda e: e.memset(eps_t[:, 1:2], LN_EPS), writes=[r_eps])
        act_preload(AF.Ln)

        gseq = []
        for _ti in range(len(cfg.tiles)):
            for l in range(NL):
                for gi in range(len(cfg.groups)):
                    gseq.append((l, gi))
        ws = {"issued": 0, "next": 0}

        def ws_issue(after=()):
            g = ws["issued"]
            if g >= len(gseq):
                return
            l, gi = gseq[g]
            _, _, ncols, kcm = cfg.groups[gi]
            L = ncols * kcm
            off = cfg.goffs[gi]
            s = g % cfg.NSLOT
            S.add("pool", lambda e, s=s, l=l, off=off, L=L: e.dma_start(out=slots[s][:, 0:L], in_=wts[l, :, off:off + L]),
                  reads=list(after), writes=[r_slot[s]], chan=ch_slot[s])
            ws["issued"] += 1

        def ws_acquire(kind, idx):
            g = ws["next"]
            l, gi = gseq[g]
            assert cfg.groups[gi][0] == kind and cfg.groups[gi][1] == idx, (cfg.groups[gi], kind, idx)
            while ws["issued"] <= g:
                ws_issue()
            return g % cfg.NSLOT, cfg.groups[gi][2], cfg.groups[gi][3]

        def ws_release():
            ws["next"] += 1
            while ws["issued"] < min(len(gseq), ws["next"] + cfg.NSLOT):
                ws_issue()

        for _ in range(min(2, cfg.NSLOT)):
            ws_issue()

        pbrot = {"i": 0, "n": 3}

        def next_pb():
            i = pbrot["i"] % pbrot["n"]
            pbrot["i"] += 1
            return i

        def mm(pbi, s, ncols, kcm, n, in_fn, in_res, kc0=0, first=True, last=True, fine=False):
            if fine:
                in_res = list(in_res)
                for kc in range(kcm):
                    def fk(e, kc=kc):
                        li = None
                        lhsT = slots[s][:, kc * ncols + n * 128: kc * ncols + (n + 1) * 128]
                        for hh in range(2):
                            li = e.matmul(PB[pbi][:, hh, 0:H], lhsT, in_fn(kc0 + kc, hh),
                                          start=(first and kc == 0), stop=(last and kc == kcm - 1))
                        return li
                    S.add("pe", fk, reads=[r_slot[s], in_res[kc0 + kc]], writes=[r_pb[pbi]])
                return

            def fn(e):
                li = None
                for kc in range(kcm):
                    lhsT = slots[s][:, kc * ncols + n * 128: kc * ncols + (n + 1) * 128]
                    for hh in range(2):
                        li = e.matmul(PB[pbi][:, hh, 0:H], lhsT, in_fn(kc0 + kc, hh),
                                      start=(first and kc == 0), stop=(last and kc == kcm - 1))
                return li
            S.add("pe", fn, reads=[r_slot[s]] + list(in_res), writes=[r_pb[pbi]])

        def h_in(kc, hh):
            return hb[:, kc, hh * H:(hh + 1) * H]

        def blk_in(base):
            return lambda kc, hh: blk(base + kc)[:, hh * H:(hh + 1) * H]

        def stats_step(j):
            sj = j % 2
            S.add("act", lambda e, j=j, sj=sj: e.activation(out=sq[sj], in_=xT[:, j, :], func=AF.Square),
                  reads=[r_x[j]], writes=[r_sq[sj]])

            def fn(e, j=j, sj=sj):
                li = None
                for hh in range(2):
                    li = e.matmul(PB[3][:, hh, 0:H], ones_bf[:, :], sq[sj][:, hh * H:(hh + 1) * H],
                                  start=(j == 0), stop=(j == KC - 1))
                return li
            S.add("pe", fn, reads=[r_sq[sj], r_ones], writes=[r_pb[3]])

        def rms_finish(nxt=None):
            S.add("act", lambda e: e.activation(out=v2(rt[:, :]), in_=ps(PB[3]), func=AF.Ln, bias=cc("eps", 0, 0), scale=1.0),
                  reads=[r_pb[3], r_eps], writes=rA)
            S.add("act", lambda e: e.activation(out=rstd[:, :], in_=rt[:, :], func=AF.Exp, scale=-0.5), reads=rA, writes=[r_tB])
            if nxt is not None:
                act_preload(nxt)

        def rms_apply(gname, l):
            for j in range(KC):
                S.add("dve", lambda e, j=j: e.scalar_tensor_tensor(out=hb[:, j, :], in0=xT[:, j, :], scalar=cc(gname, l, j),
                                                                   in1=rstd[:, :], op0=ALU.mult, op1=ALU.mult),
                      reads=[r_x[j], r_tB, r_cst], writes=[r_h[j]])

        def sview(ap, e_):
            return ap.rearrange("p (s e) -> p s e", e=e_)

        def pre_hist_ops(l, pair):
            ops = []
            items = []
            for n_, j in enumerate(pair):
                ub = n_ % 2
                i = hsc["i"] % NDG
                hsc["i"] += 1
                items.append((j, ub, i))
                ops.append(("sp", lambda e, j=j, ub=ub: e.dma_start(out=ubs[ub][:, :], in_=sb_in[l, j, :, :]),
                            dict(writes=[r_ubs[ub]], chan=ch_ubs_l[ub])))
            for k in range(WB - 1):
                ne = min(DEC, WB - 1 - k)
                for (j, ub, i) in items:
                    wk = cc("wb", l, j * WB + k)
                    o_ap = dg[i][:, 0:ne * NS]
                    i_ap = ubs[ub][:, k * NS:(k + ne) * NS]
                    if k == 0:
                        ops.append(("dve", lambda e, o_ap=o_ap, i_ap=i_ap, wk=wk: e.tensor_scalar_mul(out=o_ap, in0=i_ap, scalar1=wk),
                                    dict(reads=[r_ubs[ub], r_cst], writes=[r_dg[i]], relax=True)))
                    else:
                        ops.append(("dve", lambda e, o_ap=o_ap, i_ap=i_ap, wk=wk: e.scalar_tensor_tensor(
                            out=o_ap, in0=i_ap, scalar=wk, in1=o_ap, op0=ALU.mult, op1=ALU.add),
                            dict(reads=[r_ubs[ub], r_cst, r_dg[i]], writes=[r_dg[i]], relax=True)))
            for (j, ub, i) in items:
                ops.append(("sp", lambda e, j=j, i=i: e.dma_start(out=hs[l, j, :, :], in_=dg[i][:, :]),
                            dict(reads=[r_dg[i]], writes=[r_hs[l][j]], chan=ch_hs_s[i])))
            return ops

        hsc = {"i": 0}

        def flush_ops(ops):
            for (eng, fn, kw) in ops:
                S.add(eng, fn, **kw)

        for ti, (TP, TS) in enumerate(cfg.tiles):
            c0 = ti * T
            NDVE = cfg.NDVE[ti]
            segs = []
            for hh in range(2):
                a_, b_ = hh * H, (hh + 1) * H
                if min(b_, TP) > a_:
                    segs.append(("p", hh, 0, min(b_, TP) - a_, a_))
                if b_ > max(a_, TP):
                    sa0 = max(a_, TP)
                    segs.append(("s", hh, sa0 - a_, b_ - a_, sa0 - TP))
            rA_used = rA if TS else rA[0:1]
            rAcc_used = rAcc if TS else rAcc[0:1]

            if ti == 0:
                for k in range(KC):
                    S.add("sp", lambda e, c0=c0, k=k: e.dma_start(out=xT[:, k, :], in_=xin[k * 128:(k + 1) * 128, c0:c0 + T]),
                          writes=[r_x[k]], chan=ch_x[k])
                while ws["issued"] < cfg.NSLOT:
                    ws_issue(after=[r_x[KC - 1]])
            pbrot["n"] = 3
            for j in range(KC):
                stats_step(j)

            for l in range(NL):
                rms_finish(AF.Sigmoid)
                rms_apply("g1", l)

                NPEt = cfg.NPE[ti]
                pbrot["n"] = 3 if NPEt else 4

                def run_a1(l, TP, TS, segs, rAcc_used):
                    def uq(ap):
                        return ap.bitcast(F32R) if NPEt else ap
                    KD = WB - NPEt

                    def a1_front(j):
                        ub = j % 2
                        A, Um = [], []
                        s, ncols, kcm = ws_acquire("A1a", j)
                        p1 = 0 if NPEt else next_pb()
                        mm(p1, s, ncols, kcm, 0, h_in, r_h, fine=(j == 0))
                        S.add("act", lambda e, p1=p1: e.activation(out=v2(tmpA[:, :]), in_=ps(PB[p1]), func=AF.Copy),
                              reads=[r_pb[p1]], writes=rA)
                        S.add("act", lambda e: e.activation(out=cvp[:, 0:2], in_=carry_a[:, (l * KC + j) * 2:(l * KC + j) * 2 + 2], func=AF.Copy),
                              reads=[r_ca], writes=[r_cvp])
                        if TS:
                            S.add("sp", lambda e: e.dma_start(out=cvs[:, :], in_=sa_in[l, j, :, :]), writes=[r_cvs], chan=ch_cvs_l)
                        p2 = 1 if NPEt else next_pb()
                        mm(p2, s, ncols, kcm, 1, h_in, r_h)
                        ws_release()
                        if not TS:
                            A.append(("dve", lambda e: e.tensor_tensor(out=v2(cvp[:, 2:2 + T]), in0=ps(PB[p2]), in1=v2(tmpA[:, :]), op=ALU.mult),
                                      dict(reads=[r_pb[p2]] + rA, writes=[r_cvp], relax=True)))
                        for (kind, hh, a0, a1, do) in (segs if TS else []):
                            n = a1 - a0
                            if kind == "p":
                                A.append(("dve", lambda e, hh=hh, a0=a0, a1=a1, do=do, n=n: e.tensor_tensor(
                                    out=cvp[:, 2 + do:2 + do + n], in0=PB[p2][:, hh, a0:a1], in1=tmpA[:, hh * H + a0:hh * H + a1], op=ALU.mult),
                                    dict(reads=[r_pb[p2]] + rA, writes=[r_cvp], relax=True)))
                            else:
                                A.append(("dve", lambda e, hh=hh, a0=a0, a1=a1, do=do, n=n: e.tensor_tensor(
                                    out=cvs[:, 2 * NS + do:2 * NS + do + n], in0=PB[p2][:, hh, a0:a1],
                                    in1=tmpA[:, hh * H + a0:hh * H + a1], op=ALU.mult),
                                    dict(reads=[r_pb[p2]] + rA, writes=[r_cvs], relax=True)))
                        for k in range(WA):
                            wk = cc("wa", l, j * WA + k)
                            if k == 0:
                                A.append(("dve", lambda e, wk=wk: e.tensor_scalar_mul(out=tmpA[:, 0:TP], in0=cvp[:, 0:TP], scalar1=wk),
                                          dict(reads=[r_cvp, r_cst], writes=[rA[0]], relax=True)))
                                if TS:
                                    A.append(("dve", lambda e, wk=wk: e.tensor_scalar_mul(
                                        out=tmpA[:, TP:T], in0=cvs[:, 0:DEC * NS], scalar1=wk),
                                        dict(reads=[r_cvs, r_cst], writes=[rA[1]], relax=True)))
                            else:
                                A.append(("dve", lambda e, wk=wk, k=k: e.scalar_tensor_tensor(
                                    out=tmpA[:, 0:TP], in0=cvp[:, k:k + TP], scalar=wk, in1=tmpA[:, 0:TP], op0=ALU.mult, op1=ALU.add),
                                    dict(reads=[r_cvp, r_cst, rA[0]], writes=[rA[0]], relax=True)))
                                if TS:
                                    A.append(("dve", lambda e, wk=wk, k=k: e.scalar_tensor_tensor(
                                        out=tmpA[:, TP:T], in0=cvs[:, k * NS:(k + DEC) * NS], scalar=wk,
                                        in1=tmpA[:, TP:T], op0=ALU.mult, op1=ALU.add),
                                        dict(reads=[r_cvs, r_cst, rA[1]], writes=[rA[1]], relax=True)))
                        s, ncols, kcm = ws_acquire("A1b", j)
                        p3 = 0 if NPEt else next_pb()
                        mm(p3, s, ncols, kcm, 0, h_in, r_h)
                        if NPEt:
                            S.add("act", lambda e: e.activation(out=v2(tmpBa[:, :]), in_=ps(PB[p3]), func=AF.Copy),
                                  reads=[r_pb[p3]], writes=r_sq)
                            A.append(("dve", lambda e: e.tensor_tensor(out=blk(j), in0=tmpBa[:, :], in1=tmpA[:, :], op=ALU.mult),
                                      dict(reads=r_sq + rA, writes=[r_u[j]], relax=True)))
                        else:
                            A.append(("dve", lambda e: e.tensor_tensor(out=v2(blk(j)), in0=ps(PB[p3]), in1=v2(tmpA[:, :]), op=ALU.mult),
                                      dict(reads=[r_pb[p3]] + rA, writes=[r_u[j]], relax=True)))
                        p5 = 2 if NPEt else next_pb()
                        mm(p5, s, ncols, kcm, 1, h_in, r_h)
                        ws_release()
                        if TS:
                            S.add("sp", lambda e: e.dma_start(out=ubs[ub][:, :], in_=sb_in[l, j, :, :]), writes=[r_ubs[ub]], chan=ch_ubs_l[ub])
                        s, ncols, kcm = ws_acquire("A1c", j)
                        p4 = 0 if NPEt else next_pb()
                        mm(p4, s, ncols, kcm, 0, h_in, r_h)
                        ws_release()
                        S.add("act", lambda e: e.activation(out=v2(tmpB[:, :]), in_=ps(PB[p4]), func=AF.Sigmoid),
                              reads=[r_pb[p4]], writes=[r_tB])
                        S.add("act", lambda e: e.activation(out=uq(ubp[ub][:, 0:30]), in_=carry_b[:, (l * KC + j) * 30:(l * KC + j) * 30 + 30], func=AF.Copy),
                              reads=[r_cb_], writes=[r_ubp[ub]])
                        if not TS:
                            Um.append(("dve", lambda e: e.tensor_tensor(out=v2(uq(ubp[ub][:, 30:30 + T])), in0=ps(PB[p5]), in1=v2(tmpB[:, :]), op=ALU.mult),
                                       dict(reads=[r_pb[p5], r_tB], writes=[r_ubp[ub]], relax=True)))
                        for (kind, hh, a0, a1, do) in (segs if TS else []):
                            n = a1 - a0
                            if kind == "p":
                                Um.append(("dve", lambda e, hh=hh, a0=a0, a1=a1, do=do, n=n: e.tensor_tensor(
                                    out=uq(ubp[ub][:, 30 + do:30 + do + n]), in0=PB[p5][:, hh, a0:a1], in1=tmpB[:, hh * H + a0:hh * H + a1], op=ALU.mult),
                                    dict(reads=[r_pb[p5], r_tB], writes=[r_ubp[ub]], relax=True)))
                            else:
                                Um.append(("dve", lambda e, hh=hh, a0=a0, a1=a1, do=do, n=n: e.tensor_tensor(
                                    out=ubs[ub][:, 30 * NS + do:30 * NS + do + n], in0=PB[p5][:, hh, a0:a1],
                                    in1=tmpB[:, hh * H + a0:hh * H + a1], op=ALU.mult),
                                    dict(reads=[r_pb[p5], r_tB], writes=[r_ubs[ub]], relax=True)))
                        return A, Um

                    def a1_ca_tail(j):
                        S.add("act", lambda e: e.activation(out=carry_a[:, (l * KC + j) * 2:(l * KC + j) * 2 + 2], in_=cvp[:, TP:TP + 2], func=AF.Copy),
                              reads=[r_cvp], writes=[r_ca])
                        if TS:
                            S.add("sp", lambda e: e.dma_start(out=ncas[l, j, :, :], in_=cvs[:, :]), reads=[r_cvs], chan=ch_cvs_s)

                    def a1_taps(j):
                        ub = j % 2
                        Y = []
                        for k in range(WB):
                            wk = cc("wb", l, j * WB + k)
                            dst, rdst = (acc, rAcc[0]) if k % 2 == 0 else (accB, rAccB)
                            o_ap, i_ap = dst[:, 0:TP], ubp[ub][:, k:k + TP]
                            if k >= KD:
                                pass
                            elif k < 2:
                                Y.append(("dve", lambda e, o_ap=o_ap, i_ap=i_ap, wk=wk: e.tensor_scalar_mul(out=o_ap, in0=i_ap, scalar1=wk),
                                          dict(reads=[r_ubp[ub], r_cst], writes=[rdst], relax=True)))
                            else:
                                Y.append(("dve", lambda e, o_ap=o_ap, i_ap=i_ap, wk=wk: e.scalar_tensor_tensor(
                                    out=o_ap, in0=i_ap, scalar=wk, in1=o_ap, op0=ALU.mult, op1=ALU.add),
                                    dict(reads=[r_ubp[ub], r_cst, rdst], writes=[rdst], relax=True)))
                            if TS and k == 0:
                                Y.append(("sp", lambda e: e.dma_start(out=acc[:, TP:T], in_=hs[l, j, :, :]),
                                          dict(reads=[r_hs[l][j]], writes=[rAcc[1]], chan=ch_hl)))
                            if TS and k >= WB - DEC:
                                e0 = WB - 1 - k
                                o_ap = acc[:, TP + e0 * NS:T]
                                i_ap = ubs[ub][:, (WB - 1) * NS:(WB - 1 + DEC - e0) * NS]
                                Y.append(("dve", lambda e, o_ap=o_ap, i_ap=i_ap, wk=wk: e.scalar_tensor_tensor(
                                    out=o_ap, in0=i_ap, scalar=wk, in1=o_ap, op0=ALU.mult, op1=ALU.add),
                                    dict(reads=[r_ubs[ub], r_cst, rAcc[1]], writes=[rAcc[1]], relax=True)))
                        return Y

                    def a1_pe_plan(j):
                        idx = []
                        for _k in range(KD, WB):
                            idx.append(dgc["i"] % NDG)
                            dgc["i"] += 1
                        return idx

                    def a1_pe_diag(j, idx, t0, t1):
                        for t in range(t0, t1):
                            k = KD + t
                            wk = cc("wb", l, j * WB + k)
                            S.add("act", lambda e, i=idx[t], wk=wk: e.activation(out=dg[i][:, :].bitcast(F32R), in_=cc("ident", 0, 0, 128), func=AF.Copy, scale=wk),
                                  reads=[r_cst], writes=[r_dg[idx[t]]])

                    def a1_pe_taps(j, idx, npre):
                        ub = j % 2
                        for t in range(WB - KD):
                            k = KD + t
                            i = idx[t]
                            if t >= npre:
                                a1_pe_diag(j, idx, t, t + 1)

                            def fn(e, i=i, k=k):
                                li = None
                                lhsT = dg[i][:, :].bitcast(F32R)
                                if TP == T:
                                    for hh in range(2):
                                        li = e.matmul(PB[3][:, hh, 0:H], lhsT, ubp[ub][:, k + hh * H:k + (hh + 1) * H].bitcast(F32R),
                                                      start=(k == KD), stop=(k == WB - 1))
                                else:
                                    li = e.matmul(PB[3][:, 0, 0:TP], lhsT, ubp[ub][:, k:k + TP].bitcast(F32R),
                                                  start=(k == KD), stop=(k == WB - 1))
                                return li
                            S.add("pe", fn, reads=[r_dg[i], r_ubp[ub]], writes=[r_pb[3]])

                    def a1_back_tail(j):
                        ub = j % 2
                        bj = cc("bb", l, j)
                        S.add("dve", lambda e: e.scalar_tensor_tensor(out=cbf(j)[:, 0:TP], in0=acc[:, 0:TP], scalar=bj, in1=accB[:, 0:TP],
                                                                      op0=ALU.add, op1=ALU.add),
                              reads=[rAcc[0], rAccB, r_cst], writes=r_cbf(j), relax=True)
                        if TS:
                            S.add("dve", lambda e: e.tensor_scalar_add(out=cbf(j)[:, TP:T], in0=acc[:, TP:T], scalar1=bj),
                                  reads=[rAcc[1], r_cst], writes=r_cbf(j), relax=True)
                        if NPEt:
                            if TP == T:
                                S.add("dve", lambda e: e.tensor_tensor(out=v2(cbf(j)), in0=ps(PB[3]), in1=v2(cbf(j)), op=ALU.add),
                                      reads=r_cbf(j) + [r_pb[3]], writes=r_cbf(j))
                            else:
                                S.add("dve", lambda e: e.tensor_tensor(out=cbf(j)[:, 0:TP], in0=PB[3][:, 0, 0:TP], in1=cbf(j)[:, 0:TP], op=ALU.add),
                                      reads=r_cbf(j) + [r_pb[3]], writes=r_cbf(j))
                        S.add("act", lambda e: e.activation(out=carry_b[:, (l * KC + j) * 30:(l * KC + j) * 30 + 30], in_=ubp[ub][:, TP:TP + 30], func=AF.Copy),
                              reads=[r_ubp[ub]], writes=[r_cb_])
                        if TS:
                            S.add("sp", lambda e: e.dma_start(out=ncbs[l, j, :, :], in_=ubs[ub][:, :]), reads=[r_ubs[ub]], chan=ch_ubs_s[ub])

                    def flush(ops):
                        for (eng, fn, kw) in ops:
                            S.add(eng, fn, **kw)

                    prevY = None
                    dgc = {"i": 0}
                    for j in range(KC + 1):
                        if j >= 1 and NPEt:
                            pidx = a1_pe_plan(j - 1)
                            npre = min(NPEt, NDG)
                            a1_pe_diag(j - 1, pidx, 0, npre)
                        A, Um = a1_front(j) if j < KC else ([], [])
                        if j >= 1 and NPEt:
                            a1_pe_taps(j - 1, pidx, npre)
                        Y = prevY if prevY is not None else []
                        X = A + Um
                        nY, nX = len(Y), len(X)
                        start = 0
                        merged, xi = [], 0
                        for yi, y in enumerate(Y):
                            merged.append(y)
                            if yi >= start and (yi - start) % 2 == 1 and xi < nX:
                                merged.append(X[xi])
                                xi += 1
                        merged.extend(X[xi:])
                        flush(merged)
                        if j < KC:
                            a1_ca_tail(j)
                        if j >= 1:
                            a1_back_tail(j - 1)
                        prevY = a1_taps(j) if j < KC else None


                run_a1(l, TP, TS, segs, rAcc_used)
                act_preload(AF.Ln)

                for j in range(KC):
                    sj = j % 2
                    S.add("act", lambda e, j=j, sj=sj: e.activation(out=sq[sj], in_=cbf(j), func=AF.Square),
                          reads=r_cbf(j), writes=[r_sq[sj]])

                    def fn(e, j=j, sj=sj):
                        li = None
                        for hh in range(2):
                            e.matmul(PB[2][:, hh, 0:H], ones_f[:, :], cbf(j)[:, hh * H:(hh + 1) * H], start=(j == 0), stop=(j == KC - 1))
                            li = e.matmul(PB[3][:, hh, 0:H], ones_bf[:, :], sq[sj][:, hh * H:(hh + 1) * H], start=(j == 0), stop=(j == KC - 1))
                        return li
                    S.add("pe", fn, reads=[r_sq[sj], r_ones] + r_cbf(j), writes=[r_pb[2], r_pb[3]])

                S.add("act", lambda e: e.activation(out=v2(meanb[:, :]), in_=ps(PB[2]), func=AF.Copy), reads=[r_pb[2]], writes=rAcc)
                S.add("act", lambda e: e.activation(out=v2(rt[:, :]), in_=ps(PB[2]), func=AF.Square), reads=[r_pb[2]], writes=rA)
                S.add("dve", lambda e: e.tensor_tensor(out=v2(rt[:, :]), in0=ps(PB[3]), in1=v2(rt[:, :]), op=ALU.subtract), reads=[r_pb[3]] + rA, writes=rA)
                S.add("dve", lambda e: e.tensor_scalar_max(out=rt[:, :], in0=rt[:, :], scalar1=0.0), reads=rA, writes=rA)
                S.add("act", lambda e: e.activation(out=rt[:, :], in_=rt[:, :], func=AF.Ln, bias=cc("eps", 0, 1), scale=1.0), reads=rA + [r_eps], writes=rA)
                S.add("act", lambda e: e.activation(out=rstd[:, :], in_=rt[:, :], func=AF.Exp, scale=-0.5), reads=rA, writes=[r_tB])
                act_preload(AF.Silu)
                for j in range(KC):
                    S.add("dve", lambda e, j=j: e.tensor_tensor(out=cbf(j), in0=cbf(j), in1=meanb[:, :], op=ALU.subtract),
                          reads=r_cbf(j) + rAcc, writes=r_cbf(j))
                    S.add("dve", lambda e, j=j: e.tensor_tensor(out=cbf(j), in0=cbf(j), in1=rstd[:, :], op=ALU.mult),
                          reads=r_cbf(j) + [r_tB], writes=r_cbf(j))
                    S.add("act", lambda e, j=j, l=l: e.activation(out=blk(KC + j), in_=cbf(j), func=AF.Silu, bias=cc("lb", l, j), scale=cc("lg", l, j)),
                          reads=r_cbf(j) + [r_cst], writes=[r_u[KC + j]])

                pbrot["n"] = 4
                ga_res = r_u[0:KC]
                z_res = r_u[KC:2 * KC]
                for j in range(KC):
                    s, ncols, kcm = ws_acquire("A3a", j)
                    p1 = next_pb()
                    mm(p1, s, ncols, kcm, 0, h_in, r_h)
                    S.add("act", lambda e, p1=p1: e.activation(out=v2(tmpA[:, :]), in_=ps(PB[p1]), func=AF.Sigmoid), reads=[r_pb[p1]], writes=rA)
                    p2 = next_pb()
                    mm(p2, s, ncols, kcm, 1, blk_in(0), ga_res)
                    ws_release()
                    S.add("dve", lambda e, p2=p2: e.tensor_tensor(out=v2(tmpA[:, :]), in0=ps(PB[p2]), in1=v2(tmpA[:, :]), op=ALU.mult),
                          reads=[r_pb[p2]] + rA, writes=rA)
                    s, ncols, kcm = ws_acquire("A3b", j)
                    p3 = next_pb()
                    mm(p3, s, ncols, kcm, 0, h_in, r_h)
                    S.add("act", lambda e, p3=p3: e.activation(out=v2(tmpB[:, :]), in_=ps(PB[p3]), func=AF.Sigmoid), reads=[r_pb[p3]], writes=[r_tB])
                    p4 = next_pb()
                    mm(p4, s, ncols, kcm, 1, blk_in(KC), z_res, fine=(j == 0))
                    ws_release()
                    S.add("dve", lambda e, p4=p4: e.tensor_tensor(out=v2(tmpB[:, :]), in0=ps(PB[p4]), in1=v2(tmpB[:, :]), op=ALU.mult),
                          reads=[r_pb[p4], r_tB], writes=[r_tB])
                    S.add("dve", lambda e, j=j: e.tensor_tensor(out=blk(2 * KC + j), in0=tmpA[:, :], in1=tmpB[:, :], op=ALU.add),
                          reads=rA + [r_tB], writes=[r_u[2 * KC + j]])

                act_preload(AF.Ln)
                pbrot["n"] = 3
                m_res = r_u[2 * KC:3 * KC]
                for q in range(KC // cfg.A4G):
                    s, ncols, kcm = ws_acquire("A4", q)
                    for jj in range(cfg.A4G):
                        j = q * cfg.A4G + jj
                        p1 = next_pb()
                        mm(p1, s, ncols, kcm, jj, blk_in(2 * KC), m_res, fine=(j == 0))
                        S.add("dve", lambda e, p1=p1, j=j: e.tensor_tensor(out=v2(xT[:, j, :]), in0=ps(PB[p1]), in1=v2(xT[:, j, :]), op=ALU.add),
                              reads=[r_pb[p1], r_x[j]], writes=[r_x[j]])
                        if j >= 2:
                            stats_step(j - 2)
                    ws_release()
                for j in range(max(KC - 2, 0), KC):
                    stats_step(j)

                rms_finish(AF.Silu)
                rms_apply("g2", l)

                pre = []
                if ti == 0 and cfg.tiles[-1][1]:
                    for j0 in range(0, KC, 2):
                        pre += pre_hist_ops(l, [j0, j0 + 1])
                npre = -(-len(pre) // FC) if pre else 0
                for f in range(FC):
                    s, ncols, kcm = ws_acquire("F1", f)
                    p1 = next_pb()
                    mm(p1, s, ncols, kcm, 0, h_in, r_h, fine=(f == 0))
                    S.add("act", lambda e, p1=p1: e.activation(out=v2(tmpA[:, :]), in_=ps(PB[p1]), func=AF.Silu), reads=[r_pb[p1]], writes=rA)
                    p2 = next_pb()
                    mm(p2, s, ncols, kcm, 1, h_in, r_h)
                    ws_release()
                    S.add("dve", lambda e, p2=p2, f=f: e.tensor_tensor(out=v2(blk(f)), in0=ps(PB[p2]), in1=v2(tmpA[:, :]), op=ALU.mult),
                          reads=[r_pb[p2]] + rA, writes=[r_u[f]])
                    if pre:
                        flush_ops(pre[f * npre:(f + 1) * npre])

                act_preload(AF.Ln)
                ff_res = r_u[0:FC]
                for j in range(KC):
                    p1 = next_pb()
                    for hf in range(2):
                        s, ncols, kcm = ws_acquire("F2", 2 * j + hf)
                        mm(p1, s, ncols, kcm, 0, blk_in(0), ff_res, kc0=hf * cfg.FCH, first=(hf == 0), last=(hf == 1), fine=(j == 0))
                        ws_release()
                    S.add("dve", lambda e, p1=p1, j=j: e.tensor_tensor(out=v2(xT[:, j, :]), in0=ps(PB[p1]), in1=v2(xT[:, j, :]), op=ALU.add),
                          reads=[r_pb[p1], r_x[j]], writes=[r_x[j]])
                    if j >= 2:
                        stats_step(j - 2)
                for j in range(max(KC - 2, 0), KC):
                    stats_step(j)

            rms_finish()
            for j in range(KC):
                S.add("dve", lambda e, j=j: e.scalar_tensor_tensor(out=yv(j), in0=xT[:, j, :], scalar=cc("gf", 0, j), in1=rstd[:, :],
                                                                   op0=ALU.mult, op1=ALU.mult),
                      reads=[r_x[j], r_tB, r_cst], writes=[r_u[2 * j], r_u[2 * j + 1]])
                S.add("sp", lambda e, c0=c0, j=j: e.dma_start(out=yout[j * 128:(j + 1) * 128, c0:c0 + T], in_=U[:, j * T:(j + 1) * T]),
                      reads=[r_u[2 * j], r_u[2 * j + 1]], chan=ch_y[j])
                if ti + 1 < len(cfg.tiles):
                    S.add("sp", lambda e, c1=c0 + T, j=j: e.dma_start(out=xT[:, j, :], in_=xin[j * 128:(j + 1) * 128, c1:c1 + T]),
                          writes=[r_x[j]], chan=ch_x[j])

        S.add("sp", lambda e: e.dma_start(out=ncap[:, :], in_=carry_a[:, :]), reads=[r_ca], chan=ch_fin)
        S.add("sp", lambda e: e.dma_start(out=ncbp[:, :], in_=carry_b[:, :]), reads=[r_cb_], chan=ch_fin)

        S.finalize()

        sem_names = ["pe", "act", "dve", "pool", "sp"]
        esem = {n: es.enter_context(nc.semaphore(f"s_{n}")) for n in sem_names}
        for c in S.chans:
            c.sem = es.enter_context(nc.semaphore(f"c_{c.name}"))
        final_waits = [(c.sem, 16 * c.n) for c in (ch_y + [ch_fin, ch_cvs_s, ch_ubs_s[0], ch_ubs_s[1]]) if c.n > 0]
        block = es.enter_context(nc.Block())

        @block.tensor
        def _(e):
            S.emit("pe", e, esem)

        @block.scalar
        def _(e):
            S.emit("act", e, esem)

        @block.vector
        def _(e):
            S.emit("dve", e, esem)

        @block.gpsimd
        def _(e):
            S.emit("pool", e, esem)

        @block.sync
        def _(e):
            S.emit("sp", e, esem, final_waits=final_waits)

    return nc


def _fm(v, KC):
    return np.ascontiguousarray(v.reshape(KC, 128).T)


def _layout_weights(cfg, w_in, w_out_a, w_out_b, w_o, w_gate, w_up, w_down):
    KC, D = cfg.KC, cfg.D
    out = np.empty((NL, 128, cfg.WTOT), np.float32)
    ar = np.arange(128)
    for l in range(NL):
        for gi, (kind, idx, ncols, kcm) in enumerate(cfg.groups):
            W = w_in[l]
            if kind == "A1a":
                j = idx
                blkm = np.concatenate([W[:, D + j * 128 + ar], W[:, 2 * D + j * 128 + ar]], axis=1)
            elif kind == "A1b":
                j = idx
                blkm = np.concatenate([W[:, j * 128 + ar], W[:, 3 * D + j * 128 + ar]], axis=1)
            elif kind == "A1c":
                j = idx
                blkm = W[:, 4 * D + j * 128 + ar]
            elif kind == "A3a":
                j = idx
                blkm = np.concatenate([W[:, 5 * D + j * 128 + ar], w_out_a[l][:, j * 128 + ar]], axis=1)
            elif kind == "A3b":
                j = idx
                blkm = np.concatenate([W[:, 6 * D + j * 128 + ar], w_out_b[l][:, j * 128 + ar]], axis=1)
            elif kind == "A4":
                blkm = w_o[l][:, idx * ncols:(idx + 1) * ncols]
            elif kind == "F1":
                blkm = np.concatenate([w_gate[l][:, idx * 128 + ar], w_up[l][:, idx * 128 + ar]], axis=1)
            else:
                j, hf = idx // 2, idx % 2
                blkm = w_down[l][hf * kcm * 128:(hf + 1) * kcm * 128, j * 128:(j + 1) * 128]
            assert blkm.shape == (kcm * 128, ncols), (blkm.shape, kind)
            t = blkm.reshape(kcm, 128, ncols).transpose(1, 0, 2).reshape(128, kcm * ncols)
            o = cfg.goffs[gi]
            out[l, :, o:o + kcm * ncols] = t
    return out


def _layout_consts(cfg, norm_mix_g, conv_a_w, conv_b_w, conv_b_bias, ln_b_g, ln_b_b, norm_ffn_g, final_norm_g):
    KC = cfg.KC
    c = np.zeros((128, cfg.NCONST), np.float32)
    for l in range(NL):
        c[:, cfg.coff[("g1", l)]:][:, :KC] = _fm(norm_mix_g[l], KC)
        wa = conv_a_w[l].reshape(WA, KC, 128).transpose(2, 1, 0).reshape(128, KC * WA)
        c[:, cfg.coff[("wa", l)]:][:, :KC * WA] = wa
        wb = conv_b_w[l].reshape(WB, KC, 128).transpose(2, 1, 0).reshape(128, KC * WB)
        c[:, cfg.coff[("wb", l)]:][:, :KC * WB] = wb
        c[:, cfg.coff[("bb", l)]:][:, :KC] = _fm(conv_b_bias[l], KC)
        c[:, cfg.coff[("lg", l)]:][:, :KC] = _fm(ln_b_g[l], KC)
        c[:, cfg.coff[("lb", l)]:][:, :KC] = _fm(ln_b_b[l], KC)
        c[:, cfg.coff[("g2", l)]:][:, :KC] = _fm(norm_ffn_g[l], KC)
    c[:, cfg.coff[("gf", 0)]:][:, :KC] = _fm(final_norm_g, KC)
    c[:, cfg.coff[("ident", 0)]:][:, :128] = np.eye(128, dtype=np.float32)
    return c


def run(cfg, x_prompt, x_sample, state_conv_a, state_conv_b, norm_mix_g, w_in, conv_a_w, w_out_a,
        conv_b_w, conv_b_bias, ln_b_g, ln_b_b, w_out_b, w_o, norm_ffn_g, w_gate, w_up, w_down,
        final_norm_g, trace=False):
    f = lambda a: np.asarray(a, dtype=np.float32)
    x_prompt, x_sample, state_conv_a, state_conv_b = f(x_prompt), f(x_sample), f(state_conv_a), f(state_conv_b)
    KC, D, NS, NPC, SEQ = cfg.KC, cfg.D, cfg.NS, cfg.NPC, cfg.SEQ
    wts = _layout_weights(cfg, f(w_in), f(w_out_a), f(w_out_b), f(w_o), f(w_gate), f(w_up), f(w_down))
    cst = _layout_consts(cfg, f(norm_mix_g), f(conv_a_w), f(conv_b_w), f(conv_b_bias), f(ln_b_g), f(ln_b_b),
                         f(norm_ffn_g), f(final_norm_g))
    in_maps = []
    for c in range(NCORES):
        b, hf = c // 2, c % 2
        st = 0 if hf == 0 else SEQ - NPC
        xs = x_sample[NS * c:NS * (c + 1)].transpose(1, 0, 2).reshape(DEC * NS, D)
        xc = np.concatenate([x_prompt[b, st:st + NPC], xs], axis=0)
        xin = np.ascontiguousarray(xc.T)
        sa = np.zeros((NL, KC, 128, 10, NS), np.float32)
        sa[:, :, :, 0:2, :] = state_conv_a[:, NS * c:NS * (c + 1)].reshape(NL, NS, 2, KC, 128).transpose(0, 3, 4, 2, 1)
        sbb = np.zeros((NL, KC, 128, 38, NS), np.float32)
        sbb[:, :, :, 0:30, :] = state_conv_b[:, NS * c:NS * (c + 1)].reshape(NL, NS, 30, KC, 128).transpose(0, 3, 4, 2, 1)
        in_maps.append({"xin": xin, "sa": sa.reshape(NL, KC, 128, NS * 10), "sb": sbb.reshape(NL, KC, 128, NS * 38),
                        "wts": wts, "cst": cst})
    nc = build(cfg)
    res = run_bass_kernel_spmd(nc, in_maps, core_ids=list(range(NCORES)), **({"trace": True} if trace else {}))
    R = res.results
    y_prompt = np.empty((cfg.BATCH, SEQ, D), np.float32)
    y_sample = np.empty((cfg.NSAMP, DEC, D), np.float32)
    nca_p = np.empty((NL, cfg.BATCH, 2, D), np.float32)
    ncb_p = np.empty((NL, cfg.BATCH, 30, D), np.float32)
    nca_s = np.empty((NL, cfg.NSAMP, 2, D), np.float32)
    ncb_s = np.empty((NL, cfg.NSAMP, 30, D), np.float32)
    for c in range(NCORES):
        b, hf = c // 2, c % 2
        yt = np.asarray(R[c]["yout"]).T
        if hf == 0:
            y_prompt[b, 0:NPC] = yt[0:NPC]
        else:
            y_prompt[b, NPC:SEQ] = yt[cfg.HALO:NPC]
            a = np.asarray(R[c]["ncap"]).reshape(128, NL, KC, 2)
            nca_p[:, b] = a.transpose(1, 3, 2, 0).reshape(NL, 2, D)
            bb = np.asarray(R[c]["ncbp"]).reshape(128, NL, KC, 30)
            ncb_p[:, b] = bb.transpose(1, 3, 2, 0).reshape(NL, 30, D)
        y_sample[NS * c:NS * (c + 1)] = yt[NPC:].reshape(DEC, NS, D).transpose(1, 0, 2)
        a = np.asarray(R[c]["ncas"]).reshape(NL, KC, 128, 10, NS)[:, :, :, 8:10, :]
        nca_s[:, NS * c:NS * (c + 1)] = a.transpose(0, 4, 3, 1, 2).reshape(NL, NS, 2, D)
        bb = np.asarray(R[c]["ncbs"]).reshape(NL, KC, 128, 38, NS)[:, :, :, 8:38, :]
        ncb_s[:, NS * c:NS * (c + 1)] = bb.transpose(0, 4, 3, 1, 2).reshape(NL, NS, 30, D)
    outs = (y_prompt, y_sample, nca_p, ncb_p, nca_s, ncb_s)
    if trace:
        return outs, res
    return outs


def kernel(**inputs):
    return run(FULL, **inputs)
```
